# Optimizing a Trainium2 kernel written in Bass

```python
import math
import jax, jax.numpy as jnp
from jax import lax
import numpy as np

D_MODEL = 1024
BATCH = 4
SEQ = 4096
DEPTH = 4

N_MIXERS = 3
N_HEADS = 8
HEAD_DIM = D_MODEL // N_HEADS
ROT_DIM = HEAD_DIM // 4
ROPE_THETA = 500000.0
MOBA_BLOCK = 256
MOBA_TOPK = 3
Q_CHUNK = 16
POOL_WINDOWS = (2, 4, 8, 16)
N_POOL_GROUPS = len(POOL_WINDOWS)
POOL_GROUP_DIM = D_MODEL // N_POOL_GROUPS
CONV_WIDTH = 3
D_FF = 4 * D_MODEL
NORM_EPS = 1e-6
NEG_INF = -1e30

kernel_name = "hybrid_moba_pool_shortconv_trunk"


def rms_norm(x, g):
    xf = x.astype(jnp.float32)
    y = xf * lax.rsqrt(jnp.mean(xf * xf, axis=-1, keepdims=True) + NORM_EPS)
    return (y * g.astype(jnp.float32)).astype(x.dtype)


def rope_tables(positions):
    inv_freq = ROPE_THETA ** (-jnp.arange(0, ROT_DIM, 2, dtype=jnp.float32) / ROT_DIM)
    ang = positions.astype(jnp.float32)[..., None] * inv_freq
    return jnp.cos(ang)[:, None], jnp.sin(ang)[:, None]


def apply_partial_rope(t, cos, sin):
    half = ROT_DIM // 2
    rot = t[..., :ROT_DIM].astype(jnp.float32)
    x1, x2 = rot[..., :half], rot[..., half:]
    rotated = jnp.concatenate([x1 * cos - x2 * sin, x2 * cos + x1 * sin], axis=-1)
    return jnp.concatenate([rotated.astype(t.dtype), t[..., ROT_DIM:]], axis=-1)


def moba_attention(xn, w_qkv, w_o, cos, sin):
    b, s, _ = xn.shape
    qkv = (xn @ w_qkv).reshape(b, s, 3, N_HEADS, HEAD_DIM)
    q, k, v = [jnp.transpose(qkv[:, :, i], (0, 2, 1, 3)) for i in range(3)]
    q = apply_partial_rope(q, cos, sin)
    k = apply_partial_rope(k, cos, sin)

    n_blocks = -(-s // MOBA_BLOCK)
    top_k = min(MOBA_TOPK, n_blocks)
    pad = n_blocks * MOBA_BLOCK - s
    kp = jnp.pad(k, ((0, 0), (0, 0), (0, pad), (0, 0)))
    vp = jnp.pad(v, ((0, 0), (0, 0), (0, pad), (0, 0)))
    kb = kp.reshape(b, N_HEADS, n_blocks, MOBA_BLOCK, HEAD_DIM)
    vb = vp.reshape(b, N_HEADS, n_blocks, MOBA_BLOCK, HEAD_DIM)
    k_mean = jnp.mean(kb.astype(jnp.float32), axis=3)

    scale = 1.0 / math.sqrt(HEAD_DIM)
    bi = jnp.arange(b)[:, None, None, None]
    hi = jnp.arange(N_HEADS)[None, :, None, None]
    blk_ids = jnp.arange(n_blocks)

    def chunk(c):
        start = c * Q_CHUNK
        qc = lax.dynamic_slice_in_dim(q, start, Q_CHUNK, axis=2)
        qpos = start + jnp.arange(Q_CHUNK)
        qblk = start // MOBA_BLOCK
        gate = jnp.einsum('bhqd,bhnd->bhqn', qc.astype(jnp.float32), k_mean)
        gate = jnp.where(blk_ids < qblk, gate, NEG_INF)
        _, sel = lax.top_k(gate, top_k)
        sel_valid = sel < qblk
        k_sel = kb[bi, hi, sel]
        v_sel = vb[bi, hi, sel]
        s_sel = jnp.einsum('bhqd,bhqnkd->bhqnk', qc, k_sel).astype(jnp.float32) * scale
        s_sel = jnp.where(sel_valid[..., None], s_sel, NEG_INF)
        s_sel = s_sel.reshape(b, N_HEADS, Q_CHUNK, top_k * MOBA_BLOCK)
        k_own = lax.dynamic_slice_in_dim(kp, qblk * MOBA_BLOCK, MOBA_BLOCK, axis=2)
        v_own = lax.dynamic_slice_in_dim(vp, qblk * MOBA_BLOCK, MOBA_BLOCK, axis=2)
        s_own = jnp.einsum('bhqd,bhkd->bhqk', qc, k_own).astype(jnp.float32) * scale
        kpos = qblk * MOBA_BLOCK + jnp.arange(MOBA_BLOCK)
        s_own = jnp.where(kpos[None, :] <= qpos[:, None], s_own, NEG_INF)
        p = jax.nn.softmax(jnp.concatenate([s_sel, s_own], axis=-1), axis=-1)
        p_sel = p[..., :top_k * MOBA_BLOCK].reshape(
            b, N_HEADS, Q_CHUNK, top_k, MOBA_BLOCK).astype(v.dtype)
        p_own = p[..., top_k * MOBA_BLOCK:].astype(v.dtype)
        return (jnp.einsum('bhqnk,bhqnkd->bhqd', p_sel, v_sel)
                + jnp.einsum('bhqk,bhkd->bhqd', p_own, v_own))

    outs = lax.map(chunk, jnp.arange(s // Q_CHUNK))
    out = jnp.transpose(outs, (1, 0, 3, 2, 4)).reshape(b, s, D_MODEL)
    return out @ w_o


def pool_mixer(xn, w_groups, ls_scale):
    b, s, _ = xn.shape
    xf = xn.astype(jnp.float32).reshape(b, s, N_POOL_GROUPS, POOL_GROUP_DIM)
    cs = jnp.concatenate([jnp.zeros((b, 1, N_POOL_GROUPS, POOL_GROUP_DIM), jnp.float32),
                          jnp.cumsum(xf, axis=1)], axis=1)
    t = jnp.arange(s)
    pooled = []
    for g, w in enumerate(POOL_WINDOWS):
        c = cs[:, :, g]
        lower = jnp.pad(c[:, :s - w + 1], ((0, 0), (w - 1, 0), (0, 0)))
        count = jnp.minimum(t + 1, w).astype(jnp.float32)[None, :, None]
        pooled.append((c[:, 1:] - lower) / count)
    pooled = jnp.stack(pooled, axis=2) - xf
    y = jnp.einsum('bsgc,gcd->bsgd', pooled.astype(xn.dtype), w_groups)
    return y.reshape(b, s, D_MODEL) * ls_scale


def short_conv_mixer(xn, w_in, conv_w, w_out):
    u = xn @ w_in
    gate_b, gate_c, h = jnp.split(u, 3, axis=-1)
    z = gate_c * h
    zc = lax.conv_general_dilated(
        z, conv_w[:, None, :].astype(z.dtype), window_strides=(1,),
        padding=[(CONV_WIDTH - 1, 0)], dimension_numbers=('NWC', 'WIO', 'NWC'),
        feature_group_count=D_MODEL)
    return (gate_b * zc) @ w_out


def squared_relu_mlp(x, w_up, w_down):
    h = jax.nn.relu(x @ w_up)
    return (h * h) @ w_down


def setup_inputs(seed: int = 0) -> dict:
    key = jax.random.key(seed)
    ks = jax.random.split(key, 16)
    n_attn = len(range(0, DEPTH, N_MIXERS))
    n_pool = len(range(1, DEPTH, N_MIXERS))
    n_conv = len(range(2, DEPTH, N_MIXERS))
    f32 = jnp.float32

    def nrm(k, shape, fan_in):
        return jax.random.normal(k, shape, f32) * (fan_in ** -0.5)

    x = jax.random.normal(ks[0], (BATCH, SEQ, D_MODEL), f32)
    positions = jnp.broadcast_to(jnp.arange(SEQ, dtype=jnp.int32), (BATCH, SEQ))
    norm_mix = 1.0 + 0.02 * jax.random.normal(ks[1], (DEPTH, D_MODEL), f32)
    norm_mlp = 1.0 + 0.02 * jax.random.normal(ks[2], (DEPTH, D_MODEL), f32)
    attn_w_qkv = nrm(ks[3], (n_attn, D_MODEL, 3 * D_MODEL), D_MODEL)
    attn_w_o = nrm(ks[4], (n_attn, D_MODEL, D_MODEL), D_MODEL)
    pool_w = nrm(ks[5], (n_pool, N_POOL_GROUPS, POOL_GROUP_DIM, POOL_GROUP_DIM), POOL_GROUP_DIM)
    pool_scale = 1.0 + 0.1 * jax.random.normal(ks[6], (n_pool, D_MODEL), f32)
    conv_w_in = nrm(ks[7], (n_conv, D_MODEL, 3 * D_MODEL), D_MODEL)
    conv_w = nrm(ks[8], (n_conv, CONV_WIDTH, D_MODEL), CONV_WIDTH)
    conv_w_out = nrm(ks[9], (n_conv, D_MODEL, D_MODEL), D_MODEL)
    mlp_w_up = nrm(ks[10], (DEPTH, D_MODEL, D_FF), D_MODEL)
    mlp_w_down = nrm(ks[11], (DEPTH, D_FF, D_MODEL), D_FF)
    norm_final = 1.0 + 0.02 * jax.random.normal(ks[12], (D_MODEL,), f32)
    return {"x": x, "positions": positions, "norm_mix": norm_mix, "norm_mlp": norm_mlp,
            "attn_w_qkv": attn_w_qkv, "attn_w_o": attn_w_o,
            "pool_w": pool_w, "pool_scale": pool_scale,
            "conv_w_in": conv_w_in, "conv_w": conv_w, "conv_w_out": conv_w_out,
            "mlp_w_up": mlp_w_up, "mlp_w_down": mlp_w_down, "norm_final": norm_final}


def reference(x, positions, norm_mix, norm_mlp, attn_w_qkv, attn_w_o, pool_w, pool_scale,
              conv_w_in, conv_w, conv_w_out, mlp_w_up, mlp_w_down, norm_final):
    cos, sin = rope_tables(positions)
    i_attn = i_pool = i_conv = 0
    for i in range(DEPTH):
        xn = rms_norm(x, norm_mix[i])
        kind = i % N_MIXERS
        if kind == 0:
            y = moba_attention(xn, attn_w_qkv[i_attn], attn_w_o[i_attn], cos, sin)
            i_attn += 1
        elif kind == 1:
            y = pool_mixer(xn, pool_w[i_pool], pool_scale[i_pool])
            i_pool += 1
        else:
            y = short_conv_mixer(xn, conv_w_in[i_conv], conv_w[i_conv], conv_w_out[i_conv])
            i_conv += 1
        x = x + y
        x = x + squared_relu_mlp(rms_norm(x, norm_mlp[i]), mlp_w_up[i], mlp_w_down[i])
    return rms_norm(x, norm_final)
```

```python
import math
from contextlib import ExitStack

import numpy as np
import concourse.bass as bass
import concourse.mybir as mybir
from concourse.bass_utils import run_bass_kernel_spmd

F32 = mybir.dt.float32
BF16 = mybir.dt.bfloat16
I32 = mybir.dt.int32
ALU = mybir.AluOpType
AF = mybir.ActivationFunctionType
AX = mybir.AxisListType

D = 1024
NB = 4
S = 4096
T = 2048
NCH = 8
NT = 4
TW = 512
DFF = 4096
EPS = 1e-6
NH = 8
HD = 128
BLK = 256
HALO = 16
BIG = 30000.0


class Op:
    __slots__ = ("eng", "fn", "reads", "writes", "dma", "idx", "waits", "sig", "sem", "extra", "cc")

    def __init__(self, eng, fn, reads, writes, dma):
        self.eng = eng
        self.fn = fn
        self.reads = tuple(reads)
        self.writes = tuple(writes)
        self.dma = dma
        self.waits = []
        self.sig = None
        self.sem = None
        self.extra = ()
        self.cc = False


class Prog:
    ENGS = ("pe", "act", "dve", "pool", "sp")

    def __init__(self, nc, n_dma_sems=24, same_engine_sync=True):
        self.nc = nc
        self.ops = []
        self.n_dma_sems = n_dma_sems
        self.same_engine_sync = same_engine_sync
        self.last_real = {}
        self.dmas_since_sync = []
        self.max_pool_dma = 4

    def _add(self, o):
        o.idx = len(self.ops)
        self.ops.append(o)
        if o.fn is not None:
            if o.dma:
                self.dmas_since_sync.append(o)
            else:
                self.last_real[o.eng] = o
        return o

    def op(self, eng, fn, reads=(), writes=()):
        ps_reads = [r for r in reads if r.startswith("PS")]
        if ps_reads:
            writes = tuple(writes) + tuple(ps_reads)
        return self._add(Op(eng, fn, reads, writes, False))

    def dma(self, eng, fn, reads=(), writes=()):
        return self._add(Op(eng, fn, reads, writes, True))

    def cc(self, fn, reads=(), writes=()):
        o = Op("pool", fn, reads, writes, True)
        o.cc = True
        return self._add(o)

    def barrier(self, eng, reads):
        return self.op(eng, None, reads=reads, writes=())

    def phase_sync(self):
        deps = [o for o in self.last_real.values()] + list(self.dmas_since_sync)
        for e in self.ENGS:
            o = self.op(e, None)
            o.extra = tuple(deps)
        self.dmas_since_sync = []

    def emit(self):
        nc = self.nc
        ops = self.ops
        last_writer = {}
        readers = {}
        deps = [None] * len(ops)
        for o in ops:
            d = set()
            for r in o.reads:
                w = last_writer.get(r)
                if w is not None:
                    d.add(w)
            for w_ in o.writes:
                w = last_writer.get(w_)
                if w is not None:
                    d.add(w)
                for rd in readers.get(w_, ()):
                    d.add(rd)
            for r in o.reads:
                readers.setdefault(r, []).append(o.idx)
            for w_ in o.writes:
                last_writer[w_] = o.idx
                readers[w_] = []
            d.discard(o.idx)
            dd = set()
            for j in d:
                p = ops[j]
                if (not p.dma) and (not o.dma) and p.eng == o.eng:
                    if p.eng == "pe" or not self.same_engine_sync:
                        continue
                dd.add(j)
            for p in o.extra:
                if p.idx != o.idx and not ((not p.dma) and p.eng == o.eng):
                    dd.add(p.idx)
            deps[o.idx] = dd
        signalers = set()
        for d in deps:
            signalers |= d
        cnt = {e: 0 for e in self.ENGS}
        dma_k = 0
        dma_cnt = [0] * self.n_dma_sems
        dma_prev = [None] * self.n_dma_sems
        pool_dmas = []
        n_cc = 0
        for o in ops:
            if o.cc:
                o.sem = ("cc", n_cc)
                o.sig = 1
                n_cc += 1
                continue
            if o.dma and o.eng == "pool":
                if len(pool_dmas) >= self.max_pool_dma:
                    deps[o.idx].add(pool_dmas[-self.max_pool_dma])
                    signalers.add(pool_dmas[-self.max_pool_dma])
                pool_dmas.append(o.idx)
            if o.dma:
                s = dma_k % self.n_dma_sems
                dma_k += 1
                o.sem = ("dma", s)
                prev = dma_prev[s]
                if prev is not None:
                    deps[o.idx].add(prev)
                dma_cnt[s] += 16
                o.sig = dma_cnt[s]
                dma_prev[s] = o.idx
            elif o.idx in signalers and o.fn is not None:
                cnt[o.eng] += 1
                o.sem = ("eng", o.eng)
                o.sig = cnt[o.eng]
        seen = {e: {} for e in self.ENGS}
        for o in ops:
            need = {}
            for j in deps[o.idx]:
                p = ops[j]
                assert p.sig is not None, (j, p.eng, o.idx)
                if need.get(p.sem, 0) < p.sig:
                    need[p.sem] = p.sig
            for sem, val in need.items():
                if seen[o.eng].get(sem, 0) < val:
                    seen[o.eng][sem] = val
                    o.waits.append((sem, val))
        self.stats = {e: sum(1 for o in ops if o.eng == e and o.fn is not None) for e in self.ENGS}
        self.stats["signals"] = dict(cnt)
        self.stats["waits"] = sum(len(o.waits) for o in ops)
        with ExitStack() as st:
            sems = {}
            for e in self.ENGS:
                sems[("eng", e)] = st.enter_context(nc.semaphore("s_" + e))
            for s in range(self.n_dma_sems):
                sems[("dma", s)] = st.enter_context(nc.semaphore("s_dma%d" % s))
            for s in range(n_cc):
                sems[("cc", s)] = st.enter_context(nc.semaphore("s_cc%d" % s))
            block = st.enter_context(nc.Block())
            per = {e: [o for o in ops if o.eng == e] for e in self.ENGS}

            def run(engobj, lst):
                for o in lst:
                    for sem, val in o.waits:
                        engobj.wait_ge(sems[sem], val)
                    if o.fn is None:
                        continue
                    ins = o.fn(engobj)
                    if o.sig is not None:
                        ins.then_inc(sems[o.sem], 16 if (o.dma and not o.cc) else 1)

            @block.tensor
            def _(e):
                run(e, per["pe"])

            @block.scalar
            def _(e):
                run(e, per["act"])

            @block.vector
            def _(e):
                run(e, per["dve"])

            @block.gpsimd
            def _(e):
                run(e, per["pool"])

            @block.sync
            def _(e):
                run(e, per["sp"])


V_NMIX = 0
V_NMLP = 32
V_NFIN = 64
V_PSC = 72
V_CW = 80
V_INVF = 104
V_SGN = 105
V_HALF = 106
NVEC = 107

ARENA_BYTES = 196608
XOFF = 0
POFF = 65536


class Arena:
    def __init__(self, t):
        self.t = t

    def view(self, off, shape, dtype):
        esz = 2 if dtype == BF16 else 4
        n = 1
        for s_ in shape:
            n *= s_
        assert off % 4 == 0 and (n * esz) % 4 == 0
        assert off + n * esz <= ARENA_BYTES, (off, n, esz)
        a = self.t[:, off // 4:(off + n * esz) // 4]
        if dtype != F32:
            a = a.bitcast(dtype)
        if len(shape) == 2:
            a = a.rearrange("p (c t) -> p c t", c=shape[0])
        elif len(shape) == 3:
            a = a.rearrange("p (a b t) -> p a b t", a=shape[0], b=shape[1])
        return a


PAIRS = [[0, 1], [2, 3], [4, 5], [6, 7]]


def build_program(steps, fused=False):
    nc = bass.Bass("TRN2", target_bir_lowering=False)
    dt_in = lambda name, shape, dt=F32: nc.dram_tensor(name, list(shape), dt, kind="ExternalInput").ap()
    xT_d = dt_in("xT", [D, T])
    xpT_d = dt_in("xpT", [D, T])
    vec_d = dt_in("vec", [128, NVEC])
    need = set(steps)
    mlpL = sorted(int(x[3:]) for x in need if x.startswith("mlp"))
    w_up_d = {L: dt_in("mlp_w_up%d" % L, [D, DFF]) for L in mlpL}
    w_down_d = {L: dt_in("mlp_w_down%d" % L, [DFF, D]) for L in mlpL}
    attnI = sorted(int(x[4:]) for x in need if x.startswith("attn"))
    attn_wA_d = {a: dt_in("attn_wA%d" % a, [D, 3584]) for a in attnI}
    attn_wo_d = {a: dt_in("attn_wo%d" % a, [D, D]) for a in attnI}
    if attnI:
        pos_d = dt_in("pos", [128, 2 * T], I32)
        amask_d = dt_in("amask", [128, 768])
        caus_d = dt_in("caus", [128, 4 * TW])
        lsel_d = dt_in("lsel", [128, 16 * 128])
        ident_d = dt_in("ident", [128, 128])
        perm_d = dt_in("perm", [128, 512])
    if "conv" in need:
        conv_w_in_d = dt_in("conv_w_in", [D, 3 * D])
        conv_w_out_d = dt_in("conv_w_out", [D, D])
    if "pool" in need:
        pool_w_d = dt_in("pool_w", [4, 256, 256])
    corr_d = dt_in("corr", [128, 4 * HALO])
    outT_d = nc.dram_tensor("outT", [D, T], F32, kind="ExternalOutput").ap()

    st = ExitStack()
    with st:
        arena_t = st.enter_context(nc.sbuf_tensor("arena", [128, ARENA_BYTES // 4], F32))
        AR = Arena(arena_t)
        VEC = st.enter_context(nc.sbuf_tensor("vec_sb", [128, NVEC], F32))
        ONES = st.enter_context(nc.sbuf_tensor("ones_sb", [128, 128], BF16))
        CORR = st.enter_context(nc.sbuf_tensor("corr_sb", [128, 4 * HALO], F32))
        PS = [st.enter_context(nc.psum_tensor("ps%d" % i, [128, TW], F32)) for i in range(8)]
        P = Prog(nc)
        X = AR.view(XOFF, (NCH, T), F32)
        exch_count = [0]
        cc_count = [0]
        rope_cache = None
        if fused and attnI:
            rope_cache = (nc.dram_tensor("cos_cache", [128, 2 * T], F32).ap(),
                          nc.dram_tensor("sin_cache", [128, 2 * T], F32).ap())
        if fused:
            ccsem = st.enter_context(nc.semaphore("ccsem"))
            CCS = st.enter_context(nc.sbuf_tensor("ccs_sb", [128, 1], F32))

        def emit_allgather(src, dst, reads, writes):
            cc_count[0] += 1
            k = cc_count[0]

            def fn(e):
                e.collective_compute("AllGather", ALU.bypass, replica_groups=PAIRS, ins=[src],
                                     outs=[dst]).then_inc(ccsem, 1)
                e.wait_ge(ccsem, k)
                return e.memset(CCS[:], 0.0)

            P.op("pool", fn, reads=reads, writes=writes)

        def tl(tt):
            return slice(tt * TW, (tt + 1) * TW)

        P.dma("sp", lambda e: e.dma_start(out=VEC[:], in_=vec_d), writes=["VEC"])
        P.dma("sp", lambda e: e.dma_start(out=CORR[:], in_=corr_d), writes=["CORR"])
        P.op("pool", lambda e: e.memset(ONES[:], 1.0), writes=["ONES"])
        xsrc = xT_d.rearrange("(c p) t -> p c t", p=128)
        if not steps[0].startswith("attn"):
            for c in range(NCH):
                P.dma("sp", lambda e, c=c: e.dma_start(out=X[:, c, :], in_=xsrc[:, c, :]),
                      writes=["X.%d.%d" % (c, tt) for tt in range(NT)])
        if attnI:
            IDENT = st.enter_context(nc.sbuf_tensor("ident_sb", [128, 128], BF16))
            LSEL = st.enter_context(nc.sbuf_tensor("lsel_sb", [128, 16 * 128], BF16))
            CAUS = st.enter_context(nc.sbuf_tensor("caus_sb", [128, 4 * TW], BF16))
            AMASK = st.enter_context(nc.sbuf_tensor("amask_sb", [128, 768], F32))
            ONESF = st.enter_context(nc.sbuf_tensor("onesf_sb", [128, 128], F32))
            PERM = st.enter_context(nc.sbuf_tensor("perm_sb", [128, 512], BF16))
            P.dma("pool", lambda e: e.dma_start(out=PERM[:], in_=perm_d), writes=["PERM"])
            P.dma("pool", lambda e: e.dma_start(out=IDENT[:], in_=ident_d), writes=["IDENT"])
            P.dma("pool", lambda e: e.dma_start(out=LSEL[:], in_=lsel_d), writes=["LSEL"])
            P.dma("pool", lambda e: e.dma_start(out=CAUS[:], in_=caus_d), writes=["CAUS"])
            P.dma("sp", lambda e: e.dma_start(out=AMASK[:], in_=amask_d), writes=["AMASK"])
            P.op("pool", lambda e: e.memset(ONESF[:], 1.0), writes=["ONESF"])

        def emit_rstd(src_fn, n, sq, rstd_ap, reads_fn, tag, psb):
            for c in range(NCH):
                P.op("act", lambda e, c=c: e.activation(out=sq[:, c, 0:n], in_=src_fn(c), func=AF.Square),
                     reads=reads_fn(c), writes=["SQ.%d" % c])
            for c in range(NCH):
                P.op("pe", lambda e, c=c: e.matmul(PS[psb][:, 0:n], lhsT=ONES[:], rhs=sq[:, c, 0:n],
                                                   start=(c == 0), stop=(c == NCH - 1)),
                     reads=["SQ.%d" % c, "ONES"], writes=["PS%d" % psb])
            P.op("act", lambda e: e.activation(out=rstd_ap, in_=PS[psb][:, 0:n], func=AF.Sqrt,
                                               bias=EPSB[:, 0:1], scale=1.0 / D),
                 reads=["PS%d" % psb, "EPSB"], writes=[tag])
            P.op("dve", lambda e: e.reciprocal(out=rstd_ap, in_=rstd_ap), reads=[tag], writes=[tag])

        EPSB = st.enter_context(nc.sbuf_tensor("epsb", [128, 1], F32))
        P.op("pool", lambda e: e.memset(EPSB[:], EPS), writes=["EPSB"])

        def emit_norm_bf16(gcol, XN, SQ, RSTD, xn_tag="XN", tiles=None):
            for tt in (range(NT) if tiles is None else tiles):
                rs = RSTD[:, tt % 2, :]
                emit_rstd(lambda c, tt=tt: X[:, c, tl(tt)], TW, SQ, rs,
                          lambda c, tt=tt: ["X.%d.%d" % (c, tt)], "RSTD.%d" % (tt % 2), 7)
                for c in range(NCH):
                    P.op("dve", lambda e, c=c, tt=tt, rs=rs: e.scalar_tensor_tensor(
                        out=XN[:, c, tl(tt)], in0=X[:, c, tl(tt)], scalar=VEC[:, gcol + c:gcol + c + 1],
                        in1=rs, op0=ALU.mult, op1=ALU.mult),
                        reads=["X.%d.%d" % (c, tt), "RSTD.%d" % (tt % 2), "VEC"],
                        writes=["%s.%d.%d" % (xn_tag, c, tt)])

        def emit_mlp(L):
            o = POFF
            XN = AR.view(o, (NCH, T), BF16); o += 32768
            WR = [AR.view(o + i * 16384, (NCH, 1024), BF16) for i in range(4)]; o += 65536
            H = [AR.view(o + i * 8192, (NCH, TW), BF16) for i in range(2)]; o += 16384
            SQ = AR.view(o, (NCH, TW), BF16); o += 8192
            RSTD = AR.view(o, (2, TW), F32); o += 4096
            T1 = AR.view(o, (2, TW), F32); o += 4096
            emit_norm_bf16(V_NMLP + 8 * L, XN, SQ, RSTD, tiles=[0])

            def load_q(q):
                sa, sb_ = (2 * q) % 4, (2 * q + 1) % 4
                for pc in range(4):
                    P.dma("pool", lambda e, pc=pc: e.dma_start(
                        out=WR[sa][:, :, pc * 256:(pc + 1) * 256],
                        in_=w_up_d[L][:, q * 1024 + pc * 256:q * 1024 + (pc + 1) * 256].rearrange("(kc p) f -> p kc f", p=128)),
                        writes=["WR%d.%d" % (sa, pc)])
                P.dma("pool", lambda e: e.dma_start(
                    out=WR[sb_], in_=w_down_d[L][q * 1024:(q + 1) * 1024, :].rearrange("(fc p) d -> p fc d", p=128)),
                    writes=["WR%d" % sb_])

            load_q(0)
            for q in range(4):
                if q + 1 < 4:
                    load_q(q + 1)
                sa, sb_ = (2 * q) % 4, (2 * q + 1) % 4
                for tt in range(NT):
                    if q == 0 and tt + 1 < NT:
                        emit_norm_bf16(V_NMLP + 8 * L, XN, SQ, RSTD, tiles=[tt + 1])
                    hb = (q * NT + tt) % 2
                    Hb = H[hb]
                    for fc in range(NCH):
                        pb = fc % 2
                        for kc in range(NCH):
                            P.op("pe", lambda e, fc=fc, kc=kc, pb=pb, tt=tt, sa=sa: e.matmul(
                                PS[pb][:], lhsT=WR[sa][:, kc, fc * 128:(fc + 1) * 128], rhs=XN[:, kc, tl(tt)],
                                start=(kc == 0), stop=(kc == NCH - 1)),
                                reads=["WR%d.%d" % (sa, fc // 2), "XN.%d.%d" % (kc, tt)], writes=["PS%d" % pb])
                        P.op("act", lambda e, pb=pb: e.activation(out=T1[:, pb, :], in_=PS[pb][:], func=AF.Relu),
                             reads=["PS%d" % pb], writes=["T1.%d" % pb])
                        P.op("dve", lambda e, pb=pb, fc=fc, Hb=Hb: e.tensor_tensor(
                            out=Hb[:, fc, :], in0=T1[:, pb, :], in1=T1[:, pb, :], op=ALU.mult),
                            reads=["T1.%d" % pb], writes=["H%d.%d" % (hb, fc)])
                    for dc in range(NCH):
                        pb = 2 + dc % 2
                        for fc in range(NCH):
                            P.op("pe", lambda e, fc=fc, dc=dc, pb=pb, Hb=Hb, sb_=sb_: e.matmul(
                                PS[pb][:], lhsT=WR[sb_][:, fc, dc * 128:(dc + 1) * 128], rhs=Hb[:, fc, :],
                                start=(fc == 0), stop=(fc == NCH - 1)),
                                reads=["WR%d" % sb_, "H%d.%d" % (hb, fc)], writes=["PS%d" % pb])
                        P.op("dve", lambda e, dc=dc, pb=pb, tt=tt: e.tensor_tensor(
                            out=X[:, dc, tl(tt)], in0=X[:, dc, tl(tt)], in1=PS[pb][:], op=ALU.add),
                            reads=["PS%d" % pb, "X.%d.%d" % (dc, tt)], writes=["X.%d.%d" % (dc, tt)])
            P.phase_sync()

        def emit_final():
            o = POFF
            SQ = AR.view(o, (NCH, TW), BF16); o += 8192
            RSTD = AR.view(o, (2, TW), F32); o += 4096
            OT = AR.view(o, (4, TW), F32); o += 8192
            osrc = outT_d.rearrange("(c p) t -> p c t", p=128)
            k = 0
            for tt in range(NT):
                rs = RSTD[:, tt % 2, :]
                emit_rstd(lambda c, tt=tt: X[:, c, tl(tt)], TW, SQ, rs,
                          lambda c, tt=tt: ["X.%d.%d" % (c, tt)], "RSTD.%d" % (tt % 2), 7)
                for c in range(NCH):
                    ob = k % 4
                    k += 1
                    P.op("dve", lambda e, c=c, tt=tt, rs=rs, ob=ob: e.scalar_tensor_tensor(
                        out=OT[:, ob, :], in0=X[:, c, tl(tt)], scalar=VEC[:, V_NFIN + c:V_NFIN + c + 1],
                        in1=rs, op0=ALU.mult, op1=ALU.mult),
                        reads=["X.%d.%d" % (c, tt), "RSTD.%d" % (tt % 2), "VEC"], writes=["OT.%d" % ob])
                    P.dma("sp", lambda e, c=c, tt=tt, ob=ob: e.dma_start(out=osrc[:, c, tl(tt)], in_=OT[:, ob, :]),
                          reads=["OT.%d" % ob], writes=["OUT.%d.%d" % (c, tt)])
            P.barrier("sp", ["OUT.%d.%d" % (c, tt) for c in range(NCH) for tt in range(NT)])

        def emit_store_x():
            osrc = outT_d.rearrange("(c p) t -> p c t", p=128)
            for c in range(NCH):
                P.dma("sp", lambda e, c=c: e.dma_start(out=osrc[:, c, :], in_=X[:, c, :]),
                      reads=["X.%d.%d" % (c, tt) for tt in range(NT)], writes=["OUTX.%d" % c])
            P.barrier("sp", ["OUTX.%d" % c for c in range(NCH)])

        def emit_halo_norm(gcol, XH, XNH, SQ, RSH, out_dtype_tag):
            if fused:
                k = exch_count[0]
                exch_count[0] += 1
                hx = nc.dram_tensor("hx%d" % k, [D, HALO], F32).ap()
                hall = nc.dram_tensor("hall%d" % k, [2 * D, HALO], F32).ap()
                P.dma("sp", lambda e: e.dma_start(out=hx.rearrange("(c p) t -> p c t", p=128), in_=X[:, :, T - HALO:T]),
                      reads=["X.%d.%d" % (c, NT - 1) for c in range(NCH)], writes=["hx%d" % k])
                emit_allgather(hx, hall, ["hx%d" % k], ["hall%d" % k])
                hsrc = hall[0:D, :].rearrange("(c p) t -> p c t", p=128)
                P.dma("sp", lambda e: e.dma_start(out=XH, in_=hsrc), reads=["hall%d" % k], writes=["XH"])
            else:
                hsrc = xpT_d.rearrange("(c p) t -> p c t", p=128)
                P.dma("sp", lambda e: e.dma_start(out=XH, in_=hsrc[:, :, T - HALO:T]), writes=["XH"])
            emit_rstd(lambda c: XH[:, c, :], HALO, SQ, RSH, lambda c: ["XH"], "RSH", 7)
            P.op("dve", lambda e: e.tensor_scalar(out=RSH, in0=RSH, scalar1=VEC[:, V_HALF:V_HALF + 1], scalar2=None,
                                                  op0=ALU.mult), reads=["RSH", "VEC"], writes=["RSH"])
            for c in range(NCH):
                P.op("dve", lambda e, c=c: e.scalar_tensor_tensor(
                    out=XNH[:, c, :], in0=XH[:, c, :], scalar=VEC[:, gcol + c:gcol + c + 1],
                    in1=RSH, op0=ALU.mult, op1=ALU.mult),
                    reads=["XH", "RSH", "VEC"], writes=["XNH.%d" % c])

        def emit_conv():
            o = POFF
            XN = AR.view(o, (NCH, T), BF16); o += 32768
            V = AR.view(o, (NCH, T), BF16); o += 32768
            WO = AR.view(o, (NCH, 1024), BF16); o += 16384
            WC = [AR.view(o + i * 6144, (3, NCH, 128), BF16) for i in range(2)]; o += 12288
            Z = [AR.view(o + i * 8256, (1, HALO + T), F32) for i in range(2)]; o += 16512
            SQ = AR.view(o, (NCH, TW), BF16)
            ZC = AR.view(o, (2, TW), F32)
            TH = AR.view(o + 4096, (2, TW), F32); o += 8192
            RSTD = AR.view(o, (2, TW), F32); o += 4096
            XH = AR.view(o, (NCH, HALO), F32); o += 512
            XNH = AR.view(o, (NCH, HALO), BF16); o += 256
            RSH = AR.view(o, (1, HALO), F32)[:, 0, :]; o += 64
            gcol = V_NMIX + 8 * 2
            emit_norm_bf16(gcol, XN, SQ, RSTD)
            emit_halo_norm(gcol, XH, XNH, SQ, RSH, BF16)
            P.phase_sync()
            P.dma("pool", lambda e: e.dma_start(out=WO, in_=conv_w_out_d.rearrange("(kc p) d -> p kc d", p=128)),
                  writes=["WO"])
            wsrc = conv_w_in_d.rearrange("(kc p) f -> p kc f", p=128)

            def load_wc(c):
                for j in range(3):
                    P.dma("pool", lambda e, c=c, j=j: e.dma_start(
                        out=WC[c % 2][:, j, :, :], in_=wsrc[:, :, j * 1024 + c * 128:j * 1024 + (c + 1) * 128]),
                        writes=["WC%d.%d" % (c % 2, j)])

            load_wc(0)
            for c in range(NCH):
                if c + 1 < NCH:
                    load_wc(c + 1)
                W = WC[c % 2]
                wtag = "WC%d" % (c % 2)
                Zb = Z[c % 2][:, 0, :]
                ztag = "Z%d" % (c % 2)
                for j, pb in ((1, 4), (2, 5)):
                    for kc in range(NCH):
                        P.op("pe", lambda e, j=j, pb=pb, kc=kc, W=W: e.matmul(
                            PS[pb][:, 0:HALO], lhsT=W[:, j, kc, :], rhs=XNH[:, kc, :],
                            start=(kc == 0), stop=(kc == NCH - 1)),
                            reads=[wtag + ".%d" % j, "XNH.%d" % kc], writes=["PS%d" % pb])
                P.op("act", lambda e: e.activation(out=TH[:, 0, 0:HALO], in_=PS[5][:, 0:HALO], func=AF.Copy),
                     reads=["PS5"], writes=["TH.0"])
                P.op("dve", lambda e, Zb=Zb: e.tensor_tensor(out=Zb[:, 0:HALO], in0=TH[:, 0, 0:HALO],
                                                             in1=PS[4][:, 0:HALO], op=ALU.mult),
                     reads=["TH.0", "PS4"], writes=[ztag + ".h"])
                for tt in range(NT):
                    pbb = 0 + tt % 2
                    pbc = 2 + tt % 2
                    pbh = 4 + tt % 2
                    for j, pb in ((0, pbb), (1, pbc), (2, pbh)):
                        for kc in range(NCH):
                            P.op("pe", lambda e, j=j, pb=pb, kc=kc, W=W, tt=tt: e.matmul(
                                PS[pb][:], lhsT=W[:, j, kc, :], rhs=XN[:, kc, tl(tt)],
                                start=(kc == 0), stop=(kc == NCH - 1)),
                                reads=[wtag + ".%d" % j, "XN.%d.%d" % (kc, tt)], writes=["PS%d" % pb])
                    tb = tt % 2
                    zs = slice(HALO + tt * TW, HALO + (tt + 1) * TW)
                    P.op("act", lambda e, pbh=pbh, tb=tb: e.activation(out=TH[:, tb, :], in_=PS[pbh][:], func=AF.Copy),
                         reads=["PS%d" % pbh], writes=["TH.%d" % tb])
                    P.op("dve", lambda e, Zb=Zb, zs=zs, tb=tb, pbc=pbc: e.tensor_tensor(
                        out=Zb[:, zs], in0=TH[:, tb, :], in1=PS[pbc][:], op=ALU.mult),
                        reads=["TH.%d" % tb, "PS%d" % pbc], writes=["%s.%d" % (ztag, tt)])
                    prev = ztag + (".h" if tt == 0 else ".%d" % (tt - 1))
                    cw = lambda k, c=c: VEC[:, V_CW + 8 * k + c:V_CW + 8 * k + c + 1]
                    z0 = slice(HALO + tt * TW, HALO + (tt + 1) * TW)
                    z1 = slice(HALO + tt * TW - 1, HALO + (tt + 1) * TW - 1)
                    z2 = slice(HALO + tt * TW - 2, HALO + (tt + 1) * TW - 2)
                    P.op("dve", lambda e, Zb=Zb, z0=z0, tb=tb, cw=cw: e.tensor_scalar(
                        out=ZC[:, tb, :], in0=Zb[:, z0], scalar1=cw(2), scalar2=None, op0=ALU.mult),
                        reads=["%s.%d" % (ztag, tt), "VEC"], writes=["ZC.%d" % tb])
                    P.op("dve", lambda e, Zb=Zb, z1=z1, tb=tb, cw=cw: e.scalar_tensor_tensor(
                        out=ZC[:, tb, :], in0=Zb[:, z1], scalar=cw(1), in1=ZC[:, tb, :], op0=ALU.mult, op1=ALU.add),
                        reads=["%s.%d" % (ztag, tt), prev, "ZC.%d" % tb, "VEC"], writes=["ZC.%d" % tb])
                    P.op("dve", lambda e, Zb=Zb, z2=z2, tb=tb, cw=cw: e.scalar_tensor_tensor(
                        out=ZC[:, tb, :], in0=Zb[:, z2], scalar=cw(0), in1=ZC[:, tb, :], op0=ALU.mult, op1=ALU.add),
                        reads=["%s.%d" % (ztag, tt), prev, "ZC.%d" % tb, "VEC"], writes=["ZC.%d" % tb])
                    P.op("dve", lambda e, c=c, tt=tt, tb=tb, pbb=pbb: e.tensor_tensor(
                        out=V[:, c, tl(tt)], in0=ZC[:, tb, :], in1=PS[pbb][:], op=ALU.mult),
                        reads=["ZC.%d" % tb, "PS%d" % pbb], writes=["V.%d.%d" % (c, tt)])
            for tt in range(NT):
                for dc in range(NCH):
                    pb = 6 + dc % 2
                    for kc in range(NCH):
                        P.op("pe", lambda e, dc=dc, kc=kc, pb=pb, tt=tt: e.matmul(
                            PS[pb][:], lhsT=WO[:, kc, dc * 128:(dc + 1) * 128], rhs=V[:, kc, tl(tt)],
                            start=(kc == 0), stop=(kc == NCH - 1)),
                            reads=["WO", "V.%d.%d" % (kc, tt)], writes=["PS%d" % pb])
                    P.op("dve", lambda e, dc=dc, pb=pb, tt=tt: e.tensor_tensor(
                        out=X[:, dc, tl(tt)], in0=X[:, dc, tl(tt)], in1=PS[pb][:], op=ALU.add),
                        reads=["PS%d" % pb, "X.%d.%d" % (dc, tt)], writes=["X.%d.%d" % (dc, tt)])
            P.phase_sync()

        def emit_pool():
            o = POFF
            PO = AR.view(o, (NCH, T), BF16); o += 32768
            PW = AR.view(o, (4, 2, 256), BF16); o += 4096
            SQ = AR.view(o, (NCH, TW), BF16); o += 8192
            RS = AR.view(o, (1, T), F32)[:, 0, :]; o += 8192
            XF = [AR.view(o + i * 8256, (1, HALO + T), F32)[:, 0, :] for i in range(2)]; o += 16512
            SA = AR.view(o, (1, HALO + T), F32)[:, 0, :]; o += 8256
            SB = AR.view(o, (1, HALO + T), F32)[:, 0, :]; o += 8256
            XH = AR.view(o, (NCH, HALO), F32); o += 512
            XNH = AR.view(o, (NCH, HALO), F32); o += 512
            RSH = AR.view(o, (1, HALO), F32)[:, 0, :]; o += 64
            gcol = V_NMIX + 8 * 1
            P.dma("pool", lambda e: e.dma_start(out=PW, in_=pool_w_d.rearrange("g (ci p) d -> p g ci d", p=128)),
                  writes=["PW"])
            for tt in range(NT):
                emit_rstd(lambda c, tt=tt: X[:, c, tl(tt)], TW, SQ, RS[:, tl(tt)],
                          lambda c, tt=tt: ["X.%d.%d" % (c, tt)], "RS.%d" % tt, 7)
            emit_halo_norm(gcol, XH, XNH, SQ, RSH, F32)
            rs_all = ["RS.%d" % tt for tt in range(NT)]
            for c in range(NCH):
                g = c // 2
                w = 2 << g
                xf = XF[c % 2]
                xtag = "XF%d" % (c % 2)
                P.op("dve", lambda e, c=c, xf=xf: e.scalar_tensor_tensor(
                    out=xf[:, HALO:], in0=X[:, c, :], scalar=VEC[:, gcol + c:gcol + c + 1], in1=RS,
                    op0=ALU.mult, op1=ALU.mult),
                    reads=["X.%d.%d" % (c, tt) for tt in range(NT)] + rs_all + ["VEC"], writes=[xtag])
                P.op("dve", lambda e, c=c, xf=xf: e.tensor_copy(out=xf[:, 0:HALO], in_=XNH[:, c, :]),
                     reads=["XNH.%d" % c], writes=[xtag + "h"])
                src, stag = xf, None
                bufs = [(SA, "SA"), (SB, "SB")]
                s_ = 1
                k = 0
                n = HALO + T
                while s_ < w:
                    dst, dtag = bufs[k % 2]
                    k += 1
                    rd = [xtag, xtag + "h"] if stag is None else [stag]
                    P.op("dve", lambda e, src=src, dst=dst, s_=s_: e.tensor_tensor(
                        out=dst[:, s_:n], in0=src[:, s_:n], in1=src[:, 0:n - s_], op=ALU.add),
                        reads=rd, writes=[dtag])
                    src, stag = dst, dtag
                    s_ *= 2
                P.op("dve", lambda e, src=src, g=g: e.tensor_tensor(
                    out=src[:, HALO:2 * HALO], in0=src[:, HALO:2 * HALO], in1=CORR[:, g * HALO:(g + 1) * HALO],
                    op=ALU.mult), reads=[stag, "CORR"], writes=[stag])
                P.op("dve", lambda e, src=src, c=c, xf=xf, w=w: e.scalar_tensor_tensor(
                    out=PO[:, c, :], in0=src[:, HALO:], scalar=1.0 / w, in1=xf[:, HALO:],
                    op0=ALU.mult, op1=ALU.subtract),
                    reads=[stag, xtag], writes=["PO.%d" % c])
            for tt in range(NT):
                for g in range(4):
                    for dj in range(2):
                        dc = 2 * g + dj
                        pb = dc % 2
                        for ci in range(2):
                            P.op("pe", lambda e, g=g, dj=dj, ci=ci, pb=pb, tt=tt: e.matmul(
                                PS[pb][:], lhsT=PW[:, g, ci, dj * 128:(dj + 1) * 128], rhs=PO[:, 2 * g + ci, tl(tt)],
                                start=(ci == 0), stop=(ci == 1)),
                                reads=["PW", "PO.%d" % (2 * g + ci)], writes=["PS%d" % pb])
                        P.op("dve", lambda e, dc=dc, pb=pb, tt=tt: e.scalar_tensor_tensor(
                            out=X[:, dc, tl(tt)], in0=PS[pb][:], scalar=VEC[:, V_PSC + dc:V_PSC + dc + 1],
                            in1=X[:, dc, tl(tt)], op0=ALU.mult, op1=ALU.add),
                            reads=["PS%d" % pb, "X.%d.%d" % (dc, tt), "VEC"], writes=["X.%d.%d" % (dc, tt)])
            P.phase_sync()


        def emit_attn(ai):
            L = 0 if ai == 0 else 3
            gcol = V_NMIX + 8 * L
            wA = attn_wA_d[ai].rearrange("(kc p) f -> p kc f", p=128)
            AO = AR.view(0, (NCH, T), BF16)
            KT = AR.view(32768, (1, 2 * T), BF16)[:, 0, :]
            VV = AR.view(40960, (32, 128), BF16)
            QT = AR.view(49152, (1, T), BF16)[:, 0, :]
            T2 = AR.view(53248, (2, TW), BF16)
            PT = AR.view(55296, (6, TW), BF16)
            go = 61440
            GS = AR.view(go, (16, 16), F32); go += 1024
            A1 = AR.view(go, (16, 16), F32); go += 1024
            MX = AR.view(go, (16, 8), F32); go += 512
            SBB = AR.view(go, (16, 16), BF16); go += 512
            KM = AR.view(go, (1, 16), F32)[:, 0, :]; go += 64
            KMR = AR.view(go, (1, 16), F32)[:, 0, :]; go += 64
            KMH = AR.view(go, (1, 16), BF16)[:, 0, :]; go += 32
            KML = AR.view(go, (1, 16), BF16)[:, 0, :]; go += 32
            XN = AR.view(65536, (NCH, T), BF16)
            XNP = AR.view(98304, (NCH, T), BF16)
            COS = AR.view(131072, (1, 2 * T), F32)[:, 0, :]
            SIN = AR.view(147456, (1, 2 * T), F32)[:, 0, :]
            XS = AR.view(163840, (NCH, TW), F32)
            SQ = AR.view(180224, (NCH, TW), BF16)
            RSTD = AR.view(188416, (2, TW), F32)
            SELB = AR.view(192512, (1, T), BF16)[:, 0, :]

            rope_ops = []

            def Q(eng, fn, reads=(), writes=()):
                rope_ops.append((False, eng, fn, tuple(reads), tuple(writes)))

            def QD(eng, fn, reads=(), writes=()):
                rope_ops.append((True, eng, fn, tuple(reads), tuple(writes)))

            def rope_emit(n):
                for _ in range(n):
                    if not rope_ops:
                        return
                    isd, eng, fn, r_, w_ = rope_ops.pop(0)
                    (P.dma if isd else P.op)(eng, fn, r_, w_)

            n2 = 2 * T
            ANG = AR.view(0, (1, n2), F32)[:, 0, :]
            AA = AR.view(16384, (1, n2), F32)[:, 0, :]
            POSI = AR.view(16384, (1, n2), I32)[:, 0, :]
            KF = AR.view(32768, (1, n2), F32)[:, 0, :]
            KI = AR.view(49152, (1, n2), I32)[:, 0, :]
            MK = AR.view(49152, (1, n2), F32)[:, 0, :]
            C1 = 6.28125
            C2 = 2 * math.pi - 6.28125
            use_cache = fused and ai > 0 and ("attn0" in need)
            if use_cache:
                QD("sp", lambda e: e.dma_start(out=COS, in_=rope_cache[0]), writes=["COS"])
                QD("sp", lambda e: e.dma_start(out=SIN, in_=rope_cache[1]), writes=["SIN"])
            QD("sp", lambda e: e.dma_start(out=POSI, in_=pos_d), writes=["AA"]) if not use_cache else None
            if not use_cache:
                Q("dve", lambda e: e.tensor_copy(out=ANG, in_=POSI), reads=["AA"], writes=["ANG"])
                Q("dve", lambda e: e.tensor_scalar(out=ANG, in0=ANG, scalar1=VEC[:, V_INVF:V_INVF + 1], scalar2=None,
                                                      op0=ALU.mult), reads=["ANG", "VEC"], writes=["ANG"])
            for shift, OUT, otag in (((0.0, SIN, "SIN"), (math.pi / 2, COS, "COS")) if not use_cache else ()):
                Q("dve", lambda e, shift=shift: e.tensor_scalar(out=AA, in0=ANG, scalar1=shift, scalar2=None,
                                                                   op0=ALU.add), reads=["ANG"], writes=["AA"])
                Q("dve", lambda e: e.tensor_scalar(out=KF, in0=AA, scalar1=1.0 / (2 * math.pi), scalar2=None,
                                                      op0=ALU.mult), reads=["AA"], writes=["KF"])
                Q("dve", lambda e: e.tensor_copy(out=KI, in_=KF), reads=["KF"], writes=["KI"])
                Q("dve", lambda e: e.tensor_copy(out=KF, in_=KI), reads=["KI"], writes=["KF"])
                Q("dve", lambda e: e.scalar_tensor_tensor(out=AA, in0=KF, scalar=-C1, in1=AA, op0=ALU.mult,
                                                             op1=ALU.add), reads=["KF", "AA"], writes=["AA"])
                Q("dve", lambda e: e.scalar_tensor_tensor(out=AA, in0=KF, scalar=-C2, in1=AA, op0=ALU.mult,
                                                             op1=ALU.add), reads=["KF", "AA"], writes=["AA"])
                Q("dve", lambda e: e.tensor_scalar(out=MK, in0=AA, scalar1=math.pi, scalar2=None, op0=ALU.is_gt),
                     reads=["AA", "KI"], writes=["KI"])
                Q("dve", lambda e: e.scalar_tensor_tensor(out=AA, in0=MK, scalar=-2 * math.pi, in1=AA,
                                                             op0=ALU.mult, op1=ALU.add), reads=["KI", "AA"], writes=["AA"])
                Q("dve", lambda e: e.tensor_scalar(out=MK, in0=AA, scalar1=-math.pi, scalar2=None, op0=ALU.is_lt),
                     reads=["AA", "KI"], writes=["KI"])
                Q("dve", lambda e: e.scalar_tensor_tensor(out=AA, in0=MK, scalar=2 * math.pi, in1=AA,
                                                             op0=ALU.mult, op1=ALU.add), reads=["KI", "AA"], writes=["AA"])
                Q("act", lambda e, OUT=OUT: e.activation(out=OUT, in_=AA, func=AF.Sin), reads=["AA"], writes=[otag])
            if not use_cache:
                Q("dve", lambda e: e.tensor_scalar(out=SIN, in0=SIN, scalar1=VEC[:, V_SGN:V_SGN + 1], scalar2=None,
                                                      op0=ALU.mult), reads=["SIN", "VEC"], writes=["SIN"])
                if fused and ("attn1" in need):
                    QD("sp", lambda e: e.dma_start(out=rope_cache[0], in_=COS), reads=["COS"], writes=["cosd"])
                    QD("sp", lambda e: e.dma_start(out=rope_cache[1], in_=SIN), reads=["SIN"], writes=["sind"])
            def norm_stream(xs, XNd, tag):
                HW_ = 256
                XSb = [AR.view(163840 + i * 8192, (NCH, HW_), F32) for i in range(2)]
                for ht in range(2 * NT):
                    b = ht % 2
                    tt = ht // 2
                    t0 = ht * HW_
                    xsb = XSb[b]
                    P.dma("sp", lambda e, t0=t0, xsb=xsb: e.dma_start(out=xsb, in_=xs[:, :, t0:t0 + HW_]),
                          writes=["XS%d" % b])
                    rs = RSTD[:, b, 0:HW_]
                    emit_rstd(lambda c, xsb=xsb: xsb[:, c, :], HW_, SQ[:, :, b * HW_:(b + 1) * HW_], rs,
                              lambda c, b=b: ["XS%d" % b], "RSTD.%d" % b, 7 if b == 0 else 6)
                    for c in range(NCH):
                        P.op("dve", lambda e, c=c, t0=t0, rs=rs, xsb=xsb: e.scalar_tensor_tensor(
                            out=XNd[:, c, t0:t0 + HW_], in0=xsb[:, c, :], scalar=VEC[:, gcol + c:gcol + c + 1],
                            in1=rs, op0=ALU.mult, op1=ALU.mult),
                            reads=["XS%d" % b, "RSTD.%d" % b, "VEC"], writes=["%s.%d.%d" % (tag, c, tt)])
                    rope_emit(2)

            if fused and ai > 0:
                xown_d = nc.dram_tensor("xspill%d" % ai, [D, T], F32).ap()
                xall_d = nc.dram_tensor("xall%d" % ai, [NCH, 256, T], F32).ap()
                xo = xown_d.rearrange("(c p) t -> p c t", p=128)
                for c in range(NCH):
                    P.dma("sp", lambda e, c=c: e.dma_start(out=xo[:, c, :], in_=X[:, c, :]),
                          reads=["X.%d.%d" % (c, tt) for tt in range(NT)], writes=["xspill.%d" % c])
                cc_count[0] += NCH
                kfin = cc_count[0]

                def ag8(e):
                    for c in range(NCH):
                        e.collective_compute("AllGather", ALU.bypass, replica_groups=PAIRS,
                                             ins=[xown_d[c * 128:(c + 1) * 128, :]], outs=[xall_d[c]]).then_inc(ccsem, 1)
                    e.wait_ge(ccsem, kfin)
                    return e.memset(CCS[:], 0.0)

                P.op("pool", ag8, reads=["xspill.%d" % c for c in range(NCH)],
                     writes=["xall.%d" % c for c in range(NCH)])
                emit_norm_bf16(gcol, XN, SQ, RSTD)
                P.phase_sync()
                xprev3 = xall_d[:, 0:128, :].rearrange("c p t -> p c t")
            else:
                xown_d = xT_d
                xprev3 = xpT_d.rearrange("(c p) t -> p c t", p=128)
            if not (fused and ai > 0):
                norm_stream(xown_d.rearrange("(c p) t -> p c t", p=128), XN, "XN")
            norm_stream(xprev3, XNP, "XNP")
            while rope_ops:
                rope_emit(1)
            P.phase_sync()
            import os
            STOP = int(os.environ.get("ATTN_STOP", "99"))
            if STOP <= 1:
                return
            P.op("pool", lambda e: e.memset(SELB, 0.0), writes=["SELB.%d" % q4 for q4 in range(4)])
            WH = [AR.view(163840 + i * 6144, (3, NCH, 128), BF16) for i in range(2)]
            WS = AR.view(163840 + 12288, (2, NCH, 128), BF16)
            RT1s = [AR.view(163840 + 16384 + i * 2048, (1, TW), F32)[:, 0, :] for i in range(2)]
            RT2s = [AR.view(163840 + 20480 + i * 2048, (1, TW), F32)[:, 0, :] for i in range(2)]
            RDEN = AR.view(163840 + 24576, (1, TW), F32)[:, 0, :]
            pj_count = [0]
            scale = 1.0 / math.sqrt(HD)

            def load_wh(h):
                for m in range(3):
                    P.dma("pool", lambda e, h=h, m=m: e.dma_start(
                        out=WH[h % 2][:, m, :, :], in_=wA[:, :, m * 1024 + h * 128:m * 1024 + (h + 1) * 128]),
                        writes=["WH%d.%d" % (h % 2, m)])

            def load_ws(G):
                for m in range(2):
                    P.dma("pool", lambda e, G=G, m=m: e.dma_start(
                        out=WS[:, m, :, :], in_=wA[:, :, 3072 + m * 256 + G * 128:3072 + m * 256 + (G + 1) * 128]),
                        writes=["WS.%d" % m])

            def proj_A(job):
                h, m, dst, dcol, srcXN, srctag, tt, ccol, idx = job
                W = WH[h % 2]
                ba = (5, 4)[idx % 2]
                for kc in range(NCH):
                    P.op("pe", lambda e, kc=kc: e.matmul(PS[ba][:], lhsT=W[:, m, kc, :], rhs=srcXN[:, kc, tl(tt)],
                                                         start=(kc == 0), stop=(kc == NCH - 1)),
                         reads=["WH%d.%d" % (h % 2, m), "%s.%d.%d" % (srctag, kc, tt)], writes=["PS%d" % ba])
                dtag = "QT" if m == 0 else "KT.%d" % (dcol // TW)
                P.op("act", lambda e: e.activation(out=dst[:, dcol:dcol + TW], in_=PS[ba][:], func=AF.Copy),
                     reads=["PS%d" % ba], writes=[dtag])

            def proj_B(job):
                h, m, dst, dcol, srcXN, srctag, tt, ccol, idx = job
                j = h % 4
                ps_ = slice(32 * j, 32 * j + 32)
                ba = (5, 4)[idx % 2]
                bb = (6, 7)[idx % 2]
                rb = idx % 2
                RT1 = RT1s[rb]
                RT2 = RT2s[rb]
                dtag = "QT" if m == 0 else "KT.%d" % (dcol // TW)
                P.op("pe", lambda e: e.matmul(PS[bb][:], lhsT=PERM[:, j * 128:(j + 1) * 128], rhs=dst[:, dcol:dcol + TW],
                                              start=True, stop=True),
                     reads=["PERM", dtag], writes=["PS%d" % bb])
                P.op("dve", lambda e: e.tensor_tensor(out=RT1[ps_, :], in0=PS[ba][ps_, :], in1=COS[ps_, ccol:ccol + TW],
                                                      op=ALU.mult), reads=["PS%d" % ba, "COS", dtag], writes=["RT1.%d" % rb])
                P.op("dve", lambda e: e.tensor_tensor(out=RT2[ps_, :], in0=PS[bb][ps_, :], in1=SIN[ps_, ccol:ccol + TW],
                                                      op=ALU.mult), reads=["PS%d" % bb, "SIN"], writes=["RT2.%d" % rb])
                P.op("dve", lambda e: e.tensor_tensor(out=dst[ps_, dcol:dcol + TW], in0=RT1[ps_, :], in1=RT2[ps_, :],
                                                      op=ALU.add), reads=["RT1.%d" % rb, "RT2.%d" % rb, dtag], writes=[dtag])

            load_wh(0)
            for h in range(NH if STOP > 5 else 1):
                G, j = h // 4, h % 4
                if h + 1 < NH:
                    load_wh(h + 1)
                W = WH[h % 2]
                SUB = os.environ.get("ATTN_SUB", "z")
                if SUB == "a":
                    return
                jobs = []
                for k8 in range(8):
                    src, stag = (XNP, "XNP") if k8 < 4 else (XN, "XN")
                    pj_count[0] += 1
                    jobs.append((h, 1, KT, k8 * TW, src, stag, k8 % 4, k8 * TW, pj_count[0]))
                for tt in range(NT):
                    pj_count[0] += 1
                    jobs.append((h, 0, QT, tt * TW, XN, "XN", tt, (4 + tt) * TW, pj_count[0]))
                proj_A(jobs[0])
                for ji in range(1, len(jobs)):
                    proj_A(jobs[ji])
                    proj_B(jobs[ji - 1])
                proj_B(jobs[-1])
                if SUB <= "c":
                    return
                if STOP <= 2:
                    return
                kt_all = ["KT.%d" % k8 for k8 in range(8)]
                P.op("dve", lambda e: e.tensor_reduce(out=KM, in_=KT.rearrange("p (b k) -> p b k", k=BLK),
                                                      axis=AX.X, op=ALU.add), reads=kt_all, writes=["KM"])
                P.op("dve", lambda e: e.tensor_scalar(out=KMR, in0=KM, scalar1=1.0 / BLK, scalar2=None, op0=ALU.mult),
                     reads=["KM"], writes=["KMR"])
                P.op("dve", lambda e: e.tensor_copy(out=KMH, in_=KMR), reads=["KMR"], writes=["KMH"])
                P.op("dve", lambda e: e.tensor_tensor(out=KML, in0=KMR, in1=KMH, op=ALU.subtract),
                     reads=["KMR", "KMH"], writes=["KML"])
                for qt in range(16):
                    P.op("pe", lambda e, qt=qt: e.matmul(PS[7][:, qt * 16:(qt + 1) * 16],
                                                         lhsT=QT[:, qt * 128:(qt + 1) * 128], rhs=KMH,
                                                         start=True, stop=False),
                         reads=["QT", "KMH"], writes=["PS7"])
                    P.op("pe", lambda e, qt=qt: e.matmul(PS[7][:, qt * 16:(qt + 1) * 16],
                                                         lhsT=QT[:, qt * 128:(qt + 1) * 128], rhs=KML,
                                                         start=False, stop=True),
                         reads=["QT", "KML"], writes=["PS7"])
                P.op("dve", lambda e: e.tensor_tensor(out=GS, in0=PS[7][:, 0:256].rearrange("p (a b) -> p a b", a=16),
                                                      in1=AMASK[:, 0:256].rearrange("p (a b) -> p a b", a=16), op=ALU.add),
                     reads=["PS7", "AMASK"], writes=["GS"])
                for qt in range(16):
                    P.op("dve", lambda e, qt=qt: e.max(out=MX[:, qt, :], in_=GS[:, qt, :]), reads=["GS"],
                         writes=["MX.%d" % qt])
                    P.op("dve", lambda e, qt=qt: e.tensor_scalar(out=A1[:, qt, :], in0=GS[:, qt, :],
                                                                 scalar1=MX[:, qt, 2:3], scalar2=None, op0=ALU.is_ge),
                         reads=["GS", "MX.%d" % qt], writes=["A1"])
                P.op("dve", lambda e: e.tensor_tensor(out=A1, in0=A1, in1=AMASK[:, 256:512].rearrange("p (a b) -> p a b", a=16),
                                                      op=ALU.mult), reads=["A1", "AMASK"], writes=["A1"])
                P.op("dve", lambda e: e.tensor_tensor(out=A1, in0=A1, in1=AMASK[:, 512:768].rearrange("p (a b) -> p a b", a=16),
                                                      op=ALU.add), reads=["A1", "AMASK"], writes=["A1"])
                P.op("dve", lambda e: e.tensor_scalar(out=SBB, in0=A1, scalar1=-1.0, scalar2=BIG, op0=ALU.add,
                                                      op1=ALU.mult), reads=["A1"], writes=["SBB"])
                for i in range(32):
                    src, stag = (XNP, "XNP") if i < 16 else (XN, "XN")
                    to = (i % 16) * 128
                    vb = 3 if (i // 4) % 2 == 0 else 2
                    for kc in range(NCH):
                        P.op("pe", lambda e, kc=kc, i=i, src=src, to=to, W=W, vb=vb: e.matmul(
                            PS[vb][:, (i % 4) * 128:(i % 4 + 1) * 128], lhsT=src[:, kc, to:to + 128], rhs=W[:, 2, kc, :],
                            start=(kc == 0), stop=(kc == NCH - 1)),
                            reads=["WH%d.2" % (h % 2), "%s.%d.%d" % (stag, kc, (i % 16) // 4)], writes=["PS%d" % vb])
                    if i % 4 == 3:
                        g4 = i // 4
                        if g4 % 2 == 0:
                            P.op("dve", lambda e, g4=g4, vb=vb: e.tensor_copy(
                                out=VV[:, 4 * g4:4 * g4 + 4, :], in_=PS[vb][:].rearrange("p (a b) -> p a b", a=4)),
                                reads=["PS%d" % vb], writes=["VV.%d" % g4])
                        else:
                            P.op("act", lambda e, g4=g4, vb=vb: e.activation(
                                out=VV[:, 4 * g4:4 * g4 + 4, :], in_=PS[vb][:].rearrange("p (a b) -> p a b", a=4),
                                func=AF.Copy), reads=["PS%d" % vb], writes=["VV.%d" % g4])
                for q4 in range(4):
                    for r in range(4):
                        qt = 4 * q4 + r
                        P.op("pe", lambda e, qt=qt, r=r: e.matmul(PS[7][0:16, r * 128:(r + 1) * 128], lhsT=SBB[:, qt, :],
                                                                  rhs=IDENT[:], start=True, stop=True),
                             reads=["SBB", "IDENT"], writes=["PS7"])
                    P.op("act", lambda e, q4=q4: e.activation(out=SELB[0:16, q4 * TW:(q4 + 1) * TW], in_=PS[7][0:16, :],
                                                              func=AF.Copy), reads=["PS7"], writes=["SELB.%d" % q4])
                if STOP <= 3:
                    return
                items = [(tq, kt) for tq in range(NT) for kt in range(16 + 4 * (tq + 1))]
                nit = len(items)

                def emit_S(i):
                    tq, kt = items[i]
                    sbk = (0, 1, 5, 7)[i % 4]
                    m = kt - 16 - 4 * tq
                    diag = m >= 0
                    blk = kt // 2
                    P.op("pe", lambda e: e.matmul(
                        PS[sbk][:], lhsT=LSEL[:, blk * 128:(blk + 1) * 128], rhs=SELB[:, tl(tq)],
                        start=True, stop=False), reads=["LSEL", "SELB.%d" % tq], writes=["PS%d" % sbk])
                    P.op("pe", lambda e: e.matmul(
                        PS[sbk][:], lhsT=KT[:, kt * 128:(kt + 1) * 128], rhs=QT[:, tl(tq)],
                        start=False, stop=(not diag)), reads=["KT.%d" % (kt // 4), "QT"], writes=["PS%d" % sbk])
                    if diag:
                        P.op("pe", lambda e: e.matmul(
                            PS[sbk][:], lhsT=IDENT[:], rhs=CAUS[:, m * TW:(m + 1) * TW],
                            start=False, stop=True), reads=["IDENT", "CAUS"], writes=["PS%d" % sbk])
                    P.op("act", lambda e: e.activation(
                        out=PT[:, i % 6, :], in_=PS[sbk][:], func=AF.Exp, scale=scale),
                        reads=["PS%d" % sbk], writes=["PT.%d" % (i % 6)])

                def emit_PV(i):
                    tq, kt = items[i]
                    nkt = 16 + 4 * (tq + 1)
                    ob = 2 + tq % 2
                    denb = 4 if tq % 2 == 0 else 6
                    pbuf = i % 6
                    P.op("pe", lambda e: e.matmul(
                        PS[ob][:], lhsT=VV[:, kt, :], rhs=PT[:, pbuf, :], start=(kt == 0), stop=(kt == nkt - 1)),
                        reads=["VV.%d" % (kt // 4), "PT.%d" % pbuf], writes=["PS%d" % ob])
                    if kt % 2 == 1:
                        tb = (i // 2) % 2
                        P.op("dve", lambda e: e.tensor_tensor(
                            out=T2[:, tb, :], in0=PT[:, (i - 1) % 6, :], in1=PT[:, pbuf, :], op=ALU.add),
                            reads=["PT.%d" % ((i - 1) % 6), "PT.%d" % pbuf], writes=["T2.%d" % tb])
                        den_pending.append((i + 2, lambda: P.op("pe", lambda e: e.matmul(
                            PS[denb][:], lhsT=ONES[:], rhs=T2[:, tb, :], start=(kt == 1), stop=(kt == nkt - 1)),
                            reads=["ONES", "T2.%d" % tb], writes=["PS%d" % denb])))

                def emit_fin(tq):
                    ob = 2 + tq % 2
                    denb = 4 if tq % 2 == 0 else 6
                    P.op("dve", lambda e: e.reciprocal(out=RDEN, in_=PS[denb][:]), reads=["PS%d" % denb], writes=["RDEN"])
                    P.op("dve", lambda e, h=h: e.tensor_tensor(
                        out=AO[:, h, tl(tq)], in0=PS[ob][:], in1=RDEN, op=ALU.mult),
                        reads=["PS%d" % ob, "RDEN"], writes=["AO.%d.%d" % (h, tq)])

                den_pending = []
                emit_S(0)
                emit_S(1)
                emit_S(2)
                emit_S(3)
                pending = []
                for i in range(nit):
                    emit_PV(i)
                    if i + 4 < nit:
                        emit_S(i + 4)
                    while den_pending and den_pending[0][0] <= i:
                        den_pending.pop(0)[1]()
                    tq, kt = items[i]
                    if kt == 16 + 4 * (tq + 1) - 1:
                        pending.append((i + 4, tq))
                    while pending and pending[0][0] <= i:
                        emit_fin(pending.pop(0)[1])
                while den_pending:
                    den_pending.pop(0)[1]()
                while pending:
                    emit_fin(pending.pop(0)[1])
            if STOP <= 4:
                return
            P.phase_sync()
            XB = AR.view(65536, (NCH, T), F32)
            WO = AR.view(131072, (NCH, 1024), BF16)
            XS2 = AR.view(147456, (NCH, TW), F32)
            xs = xown_d.rearrange("(c p) t -> p c t", p=128)
            P.dma("pool", lambda e: e.dma_start(out=WO, in_=attn_wo_d[ai].rearrange("(h p) d -> p h d", p=128)),
                  writes=["WOA"])
            for tt in range(NT):
                P.dma("sp", lambda e, tt=tt: e.dma_start(out=XS2, in_=xs[:, :, tl(tt)]), writes=["XS2"])
                for dc in range(NCH):
                    pb = dc % 2
                    for h in range(NH):
                        P.op("pe", lambda e, dc=dc, h=h, pb=pb, tt=tt: e.matmul(
                            PS[pb][:], lhsT=WO[:, h, dc * 128:(dc + 1) * 128], rhs=AO[:, h, tl(tt)],
                            start=(h == 0), stop=(h == NH - 1)),
                            reads=["WOA", "AO.%d.%d" % (h, tt)], writes=["PS%d" % pb])
                    P.op("dve", lambda e, dc=dc, pb=pb, tt=tt: e.tensor_tensor(
                        out=XB[:, dc, tl(tt)], in0=XS2[:, dc, :], in1=PS[pb][:], op=ALU.add),
                        reads=["PS%d" % pb, "XS2"], writes=["XB.%d.%d" % (dc, tt)])
            P.phase_sync()
            for c in range(NCH):
                if c % 2 == 0:
                    P.op("dve", lambda e, c=c: e.tensor_copy(out=X[:, c, :], in_=XB[:, c, :]),
                         reads=["XB.%d.%d" % (c, tt) for tt in range(NT)],
                         writes=["X.%d.%d" % (c, tt) for tt in range(NT)])
                else:
                    P.op("act", lambda e, c=c: e.activation(out=X[:, c, :], in_=XB[:, c, :], func=AF.Copy),
                         reads=["XB.%d.%d" % (c, tt) for tt in range(NT)],
                         writes=["X.%d.%d" % (c, tt) for tt in range(NT)])
            P.phase_sync()

        for s_ in steps:
            if s_.startswith("mlp"):
                emit_mlp(int(s_[3:]))
            elif s_.startswith("attn"):
                emit_attn(int(s_[4:]))
            elif s_ == "conv":
                emit_conv()
            elif s_ == "pool":
                emit_pool()
            elif s_ == "final":
                emit_final()
            elif s_ == "storex":
                emit_store_x()
            else:
                raise ValueError(s_)
        P.phase_sync()
        P.emit()
        nc._prog_stats = P.stats
    return nc


def _vec_table(norm_mix, norm_mlp, norm_final, pool_scale, conv_w):
    v = np.zeros((128, NVEC), np.float32)

    def put(col, vec):
        v[:, col:col + 8] = np.asarray(vec, np.float32).reshape(8, 128).T

    for i in range(4):
        put(V_NMIX + 8 * i, norm_mix[i])
        put(V_NMLP + 8 * i, norm_mlp[i])
    put(V_NFIN, norm_final)
    put(V_PSC, pool_scale[0])
    for k in range(3):
        put(V_CW + 8 * k, conv_w[0, k])
    p = np.arange(128)
    v[:, V_INVF] = (500000.0 ** (-(np.arange(0, 32, 2, dtype=np.float32)) / 32.0)).astype(np.float32)[p % 16]
    v[:, V_SGN] = np.where(p % 32 < 16, -1.0, 1.0)
    v[:, V_HALF] = 1.0
    return v


def _corr_table(half):
    c = np.ones((128, 4 * HALO), np.float32)
    if half == 0:
        for g, w in enumerate((2, 4, 8, 16)):
            t = np.arange(HALO)
            c[:, g * HALO:(g + 1) * HALO] = (w / np.minimum(t + 1, w)).astype(np.float32)[None, :]
    return c


def _attn_wA(w_qkv):
    w_qkv = np.asarray(w_qkv, np.float32)
    wq = w_qkv[:, 0:1024].reshape(1024, 8, 128)
    wk = w_qkv[:, 1024:2048].reshape(1024, 8, 128)
    out = np.empty((1024, 3584), np.float32)
    for h in range(8):
        j = h % 4
        perm = np.empty(128, np.int64)
        rot = np.arange(32 * j, 32 * j + 32)
        perm[rot] = np.arange(32)
        others = np.array([p for p in range(128) if not (32 * j <= p < 32 * j + 32)])
        perm[others] = np.arange(32, 128)
        out[:, h * 128:(h + 1) * 128] = wq[:, h, perm]
        out[:, 1024 + h * 128:1024 + (h + 1) * 128] = wk[:, h, perm]
    out[:, 2048:3072] = w_qkv[:, 2048:3072]
    sw = np.concatenate([np.arange(16, 32), np.arange(0, 16)])
    for m, w in enumerate((wq, wk)):
        for G in range(2):
            for j in range(4):
                c0 = 3072 + m * 256 + G * 128 + 32 * j
                out[:, c0:c0 + 32] = w[:, 4 * G + j, sw]
    return out


def _attn_consts(half):
    am = np.zeros((128, 3, 16, 16), np.float32)
    for qt in range(16):
        own = 8 + qt // 2
        for j in range(16):
            valid = (j < own) and (half == 1 or j >= 8)
            am[:, 0, qt, j] = 0.0 if valid else -1e30
            am[:, 1, qt, j] = 1.0 if valid else 0.0
            am[:, 2, qt, j] = 1.0 if j == own else 0.0
    caus = np.zeros((128, 4, TW), np.float32)
    k = np.arange(128)[:, None]
    q = np.arange(TW)[None, :]
    for m in range(4):
        caus[:, m, :] = np.where(128 * m + k <= q, 0.0, -BIG)
    lsel = np.zeros((128, 16, 128), np.float32)
    for j in range(16):
        lsel[j, j, :] = 1.0
    perm = np.zeros((128, 4, 128), np.float32)
    for j in range(4):
        for i in range(32):
            perm[32 * j + (i + 16) % 32, j, 32 * j + i] = 1.0
    return (am.reshape(128, 768), caus.reshape(128, 4 * TW), lsel.reshape(128, 2048),
            np.eye(128, dtype=np.float32), perm.reshape(128, 512))


def run_steps(steps, xT_cores, xpT_cores, inputs, prepared=None, fused=False):
    nc = build_program(steps, fused=fused)
    vec = _vec_table(inputs["norm_mix"], inputs["norm_mlp"], inputs["norm_final"], inputs["pool_scale"],
                     inputs["conv_w"])
    prepared = prepared if prepared is not None else {}
    in_maps = []
    for core in range(8):
        half = core % 2
        b = core // 2
        vc = vec.copy()
        vc[:, V_HALF] = float(half)
        m = {"xT": xT_cores[core], "xpT": xpT_cores[core], "vec": vc, "corr": _corr_table(half)}
        for st_ in steps:
            if st_.startswith("mlp"):
                L = int(st_[3:])
                m["mlp_w_up%d" % L] = inputs["mlp_w_up"][L]
                m["mlp_w_down%d" % L] = inputs["mlp_w_down"][L]
            elif st_.startswith("attn"):
                a = int(st_[4:])
                key = "wA%d" % a
                if key not in prepared:
                    prepared[key] = _attn_wA(inputs["attn_w_qkv"][a])
                m["attn_wA%d" % a] = prepared[key]
                m["attn_wo%d" % a] = inputs["attn_w_o"][a]
                pos = np.asarray(inputs["positions"], np.int32)
                own = pos[b, half * T:(half + 1) * T]
                prev = pos[b, 0:T] if half == 1 else np.zeros(T, np.int32)
                m["pos"] = np.ascontiguousarray(np.broadcast_to(np.concatenate([prev, own])[None, :], (128, 2 * T)))
                am, caus, lsel, ident, perm = _attn_consts(half)
                m["amask"], m["caus"], m["lsel"], m["ident"], m["perm"] = am, caus, lsel, ident, perm
            elif st_ == "conv":
                m["conv_w_in"] = inputs["conv_w_in"][0]
                m["conv_w_out"] = inputs["conv_w_out"][0]
            elif st_ == "pool":
                m["pool_w"] = inputs["pool_w"][0]
        in_maps.append(m)
    res = run_bass_kernel_spmd(nc, in_maps, core_ids=list(range(8)))
    return [np.asarray(r["outT"]) for r in res.results]


ALL_STEPS = ("attn0", "mlp0", "pool", "mlp1", "conv", "mlp2", "attn1", "mlp3", "final")


def kernel(**inputs):
    inputs = {k: np.asarray(v) for k, v in inputs.items()}
    x = inputs["x"].astype(np.float32, copy=False)
    xT = [np.ascontiguousarray(x[c // 2, (c % 2) * T:(c % 2 + 1) * T, :].T) for c in range(8)]
    xpT = [xT[c - 1] if c % 2 == 1 else np.zeros_like(xT[c]) for c in range(8)]
    outs = run_steps(list(ALL_STEPS), xT, xpT, inputs, fused=True)
    out = np.empty((NB, S, D), np.float32)
    for c in range(8):
        out[c // 2, (c % 2) * T:(c % 2 + 1) * T, :] = outs[c].T
    return out
```

```python
import math
from contextlib import ExitStack

import numpy as np
import concourse.bass as bass
import concourse.mybir as mybir
from concourse.bass_utils import run_bass_kernel_spmd

F32 = mybir.dt.float32
BF16 = mybir.dt.bfloat16
I32 = mybir.dt.int32
ALU = mybir.AluOpType
AF = mybir.ActivationFunctionType
AX = mybir.AxisListType

D = 1024
NB = 4
S = 4096
T = 2048
NCH = 8
NT = 4
TW = 512
DFF = 4096
EPS = 1e-6
NH = 8
HD = 128
BLK = 256
HALO = 16
BIG = 30000.0


class Op:
    __slots__ = ("eng", "fn", "reads", "writes", "dma", "idx", "waits", "sig", "sem", "extra", "cc")

    def __init__(self, eng, fn, reads, writes, dma):
        self.eng = eng
        self.fn = fn
        self.reads = tuple(reads)
        self.writes = tuple(writes)
        self.dma = dma
        self.waits = []
        self.sig = None
        self.sem = None
        self.extra = ()
        self.cc = False


class Prog:
    ENGS = ("pe", "act", "dve", "pool", "sp")

    def __init__(self, nc, n_dma_sems=24, same_engine_sync=True):
        self.nc = nc
        self.ops = []
        self.n_dma_sems = n_dma_sems
        self.same_engine_sync = same_engine_sync
        self.last_real = {}
        self.dmas_since_sync = []
        self.max_pool_dma = 4

    def _add(self, o):
        o.idx = len(self.ops)
        self.ops.append(o)
        if o.fn is not None:
            if o.dma:
                self.dmas_since_sync.append(o)
            else:
                self.last_real[o.eng] = o
        return o

    def op(self, eng, fn, reads=(), writes=()):
        ps_reads = [r for r in reads if r.startswith("PS")]
        if ps_reads:
            writes = tuple(writes) + tuple(ps_reads)
        return self._add(Op(eng, fn, reads, writes, False))

    def dma(self, eng, fn, reads=(), writes=()):
        return self._add(Op(eng, fn, reads, writes, True))

    def cc(self, fn, reads=(), writes=()):
        o = Op("pool", fn, reads, writes, True)
        o.cc = True
        return self._add(o)

    def barrier(self, eng, reads):
        return self.op(eng, None, reads=reads, writes=())

    def phase_sync(self):
        deps = [o for o in self.last_real.values()] + list(self.dmas_since_sync)
        for e in self.ENGS:
            o = self.op(e, None)
            o.extra = tuple(deps)
        self.dmas_since_sync = []

    def emit(self):
        nc = self.nc
        ops = self.ops
        last_writer = {}
        readers = {}
        deps = [None] * len(ops)
        for o in ops:
            d = set()
            for r in o.reads:
                w = last_writer.get(r)
                if w is not None:
                    d.add(w)
            for w_ in o.writes:
                w = last_writer.get(w_)
                if w is not None:
                    d.add(w)
                for rd in readers.get(w_, ()):
                    d.add(rd)
            for r in o.reads:
                readers.setdefault(r, []).append(o.idx)
            for w_ in o.writes:
                last_writer[w_] = o.idx
                readers[w_] = []
            d.discard(o.idx)
            dd = set()
            for j in d:
                p = ops[j]
                if (not p.dma) and (not o.dma) and p.eng == o.eng:
                    if p.eng == "pe" or not self.same_engine_sync:
                        continue
                dd.add(j)
            for p in o.extra:
                if p.idx != o.idx and not ((not p.dma) and p.eng == o.eng):
                    dd.add(p.idx)
            deps[o.idx] = dd
        signalers = set()
        for d in deps:
            signalers |= d
        cnt = {e: 0 for e in self.ENGS}
        dma_k = 0
        dma_cnt = [0] * self.n_dma_sems
        dma_prev = [None] * self.n_dma_sems
        pool_dmas = []
        n_cc = 0
        for o in ops:
            if o.cc:
                o.sem = ("cc", n_cc)
                o.sig = 1
                n_cc += 1
                continue
            if o.dma and o.eng == "pool":
                if len(pool_dmas) >= self.max_pool_dma:
                    deps[o.idx].add(pool_dmas[-self.max_pool_dma])
                    signalers.add(pool_dmas[-self.max_pool_dma])
                pool_dmas.append(o.idx)
            if o.dma:
                s = dma_k % self.n_dma_sems
                dma_k += 1
                o.sem = ("dma", s)
                prev = dma_prev[s]
                if prev is not None:
                    deps[o.idx].add(prev)
                dma_cnt[s] += 16
                o.sig = dma_cnt[s]
                dma_prev[s] = o.idx
            elif o.idx in signalers and o.fn is not None:
                cnt[o.eng] += 1
                o.sem = ("eng", o.eng)
                o.sig = cnt[o.eng]
        seen = {e: {} for e in self.ENGS}
        for o in ops:
            need = {}
            for j in deps[o.idx]:
                p = ops[j]
                assert p.sig is not None, (j, p.eng, o.idx)
                if need.get(p.sem, 0) < p.sig:
                    need[p.sem] = p.sig
            for sem, val in need.items():
                if seen[o.eng].get(sem, 0) < val:
                    seen[o.eng][sem] = val
                    o.waits.append((sem, val))
        self.stats = {e: sum(1 for o in ops if o.eng == e and o.fn is not None) for e in self.ENGS}
        self.stats["signals"] = dict(cnt)
        self.stats["waits"] = sum(len(o.waits) for o in ops)
        with ExitStack() as st:
            sems = {}
            for e in self.ENGS:
                sems[("eng", e)] = st.enter_context(nc.semaphore("s_" + e))
            for s in range(self.n_dma_sems):
                sems[("dma", s)] = st.enter_context(nc.semaphore("s_dma%d" % s))
            for s in range(n_cc):
                sems[("cc", s)] = st.enter_context(nc.semaphore("s_cc%d" % s))
            block = st.enter_context(nc.Block())
            per = {e: [o for o in ops if o.eng == e] for e in self.ENGS}

            def run(engobj, lst):
                for o in lst:
                    for sem, val in o.waits:
                        engobj.wait_ge(sems[sem], val)
                    if o.fn is None:
                        continue
                    ins = o.fn(engobj)
                    if o.sig is not None:
                        ins.then_inc(sems[o.sem], 16 if (o.dma and not o.cc) else 1)

            @block.tensor
            def _(e):
                run(e, per["pe"])

            @block.scalar
            def _(e):
                run(e, per["act"])

            @block.vector
            def _(e):
                run(e, per["dve"])

            @block.gpsimd
            def _(e):
                run(e, per["pool"])

            @block.sync
            def _(e):
                run(e, per["sp"])


V_NMIX = 0
V_NMLP = 32
V_NFIN = 64
V_PSC = 72
V_CW = 80
V_INVF = 104
V_SGN = 105
V_HALF = 106
NVEC = 107

ARENA_BYTES = 196608
XOFF = 0
POFF = 65536


class Arena:
    def __init__(self, t):
        self.t = t

    def view(self, off, shape, dtype):
        esz = 2 if dtype == BF16 else 4
        n = 1
        for s_ in shape:
            n *= s_
        assert off % 4 == 0 and (n * esz) % 4 == 0
        assert off + n * esz <= ARENA_BYTES, (off, n, esz)
        a = self.t[:, off // 4:(off + n * esz) // 4]
        if dtype != F32:
            a = a.bitcast(dtype)
        if len(shape) == 2:
            a = a.rearrange("p (c t) -> p c t", c=shape[0])
        elif len(shape) == 3:
            a = a.rearrange("p (a b t) -> p a b t", a=shape[0], b=shape[1])
        return a


PAIRS = [[0, 1], [2, 3], [4, 5], [6, 7]]


def build_program(steps, fused=False):
    nc = bass.Bass("TRN2", target_bir_lowering=False)
    dt_in = lambda name, shape, dt=F32: nc.dram_tensor(name, list(shape), dt, kind="ExternalInput").ap()
    xT_d = dt_in("xT", [D, T])
    xpT_d = dt_in("xpT", [D, T])
    vec_d = dt_in("vec", [128, NVEC])
    need = set(steps)
    mlpL = sorted(int(x[3:]) for x in need if x.startswith("mlp"))
    w_up_d = {L: dt_in("mlp_w_up%d" % L, [D, DFF]) for L in mlpL}
    w_down_d = {L: dt_in("mlp_w_down%d" % L, [DFF, D]) for L in mlpL}
    attnI = sorted(int(x[4:]) for x in need if x.startswith("attn"))
    attn_wA_d = {a: dt_in("attn_wA%d" % a, [D, 3584]) for a in attnI}
    attn_wo_d = {a: dt_in("attn_wo%d" % a, [D, D]) for a in attnI}
    if attnI:
        pos_d = dt_in("pos", [128, 2 * T], I32)
        amask_d = dt_in("amask", [128, 768])
        caus_d = dt_in("caus", [128, 4 * TW])
        lsel_d = dt_in("lsel", [128, 16 * 128])
        ident_d = dt_in("ident", [128, 128])
        perm_d = dt_in("perm", [128, 512])
    if "conv" in need:
        conv_w_in_d = dt_in("conv_w_in", [D, 3 * D])
        conv_w_out_d = dt_in("conv_w_out", [D, D])
    if "pool" in need:
        pool_w_d = dt_in("pool_w", [4, 256, 256])
    corr_d = dt_in("corr", [128, 4 * HALO])
    outT_d = nc.dram_tensor("outT", [D, T], F32, kind="ExternalOutput").ap()

    st = ExitStack()
    with st:
        arena_t = st.enter_context(nc.sbuf_tensor("arena", [128, ARENA_BYTES // 4], F32))
        AR = Arena(arena_t)
        VEC = st.enter_context(nc.sbuf_tensor("vec_sb", [128, NVEC], F32))
        ONES = st.enter_context(nc.sbuf_tensor("ones_sb", [128, 128], BF16))
        CORR = st.enter_context(nc.sbuf_tensor("corr_sb", [128, 4 * HALO], F32))
        PS = [st.enter_context(nc.psum_tensor("ps%d" % i, [128, TW], F32)) for i in range(8)]
        P = Prog(nc)
        X = AR.view(XOFF, (NCH, T), F32)
        exch_count = [0]
        cc_count = [0]
        rope_cache = None
        if fused and attnI:
            rope_cache = (nc.dram_tensor("cos_cache", [128, 2 * T], F32).ap(),
                          nc.dram_tensor("sin_cache", [128, 2 * T], F32).ap())
        if fused:
            ccsem = st.enter_context(nc.semaphore("ccsem"))
            CCS = st.enter_context(nc.sbuf_tensor("ccs_sb", [128, 1], F32))

        def emit_allgather(src, dst, reads, writes):
            cc_count[0] += 1
            k = cc_count[0]

            def fn(e):
                e.collective_compute("AllGather", ALU.bypass, replica_groups=PAIRS, ins=[src],
                                     outs=[dst]).then_inc(ccsem, 1)
                e.wait_ge(ccsem, k)
                return e.memset(CCS[:], 0.0)

            P.op("pool", fn, reads=reads, writes=writes)

        def tl(tt):
            return slice(tt * TW, (tt + 1) * TW)

        P.dma("sp", lambda e: e.dma_start(out=VEC[:], in_=vec_d), writes=["VEC"])
        P.dma("sp", lambda e: e.dma_start(out=CORR[:], in_=corr_d), writes=["CORR"])
        P.op("pool", lambda e: e.memset(ONES[:], 1.0), writes=["ONES"])
        xsrc = xT_d.rearrange("(c p) t -> p c t", p=128)
        if not steps[0].startswith("attn"):
            for c in range(NCH):
                P.dma("sp", lambda e, c=c: e.dma_start(out=X[:, c, :], in_=xsrc[:, c, :]),
                      writes=["X.%d.%d" % (c, tt) for tt in range(NT)])
        if attnI:
            IDENT = st.enter_context(nc.sbuf_tensor("ident_sb", [128, 128], BF16))
            LSEL = st.enter_context(nc.sbuf_tensor("lsel_sb", [128, 16 * 128], BF16))
            CAUS = st.enter_context(nc.sbuf_tensor("caus_sb", [128, 4 * TW], BF16))
            AMASK = st.enter_context(nc.sbuf_tensor("amask_sb", [128, 768], F32))
            ONESF = st.enter_context(nc.sbuf_tensor("onesf_sb", [128, 128], F32))
            PERM = st.enter_context(nc.sbuf_tensor("perm_sb", [128, 512], BF16))
            P.dma("pool", lambda e: e.dma_start(out=PERM[:], in_=perm_d), writes=["PERM"])
            P.dma("pool", lambda e: e.dma_start(out=IDENT[:], in_=ident_d), writes=["IDENT"])
            P.dma("pool", lambda e: e.dma_start(out=LSEL[:], in_=lsel_d), writes=["LSEL"])
            P.dma("pool", lambda e: e.dma_start(out=CAUS[:], in_=caus_d), writes=["CAUS"])
            P.dma("sp", lambda e: e.dma_start(out=AMASK[:], in_=amask_d), writes=["AMASK"])
            P.op("pool", lambda e: e.memset(ONESF[:], 1.0), writes=["ONESF"])

        def emit_rstd(src_fn, n, sq, rstd_ap, reads_fn, tag, psb):
            for c in range(NCH):
                P.op("act", lambda e, c=c: e.activation(out=sq[:, c, 0:n], in_=src_fn(c), func=AF.Square),
                     reads=reads_fn(c), writes=["SQ.%d" % c])
            for c in range(NCH):
                P.op("pe", lambda e, c=c: e.matmul(PS[psb][:, 0:n], lhsT=ONES[:], rhs=sq[:, c, 0:n],
                                                   start=(c == 0), stop=(c == NCH - 1)),
                     reads=["SQ.%d" % c, "ONES"], writes=["PS%d" % psb])
            P.op("act", lambda e: e.activation(out=rstd_ap, in_=PS[psb][:, 0:n], func=AF.Sqrt,
                                               bias=EPSB[:, 0:1], scale=1.0 / D),
                 reads=["PS%d" % psb, "EPSB"], writes=[tag])
            P.op("dve", lambda e: e.reciprocal(out=rstd_ap, in_=rstd_ap), reads=[tag], writes=[tag])

        EPSB = st.enter_context(nc.sbuf_tensor("epsb", [128, 1], F32))
        P.op("pool", lambda e: e.memset(EPSB[:], EPS), writes=["EPSB"])

        def emit_norm_bf16(gcol, XN, SQ, RSTD, xn_tag="XN", tiles=None):
            for tt in (range(NT) if tiles is None else tiles):
                rs = RSTD[:, tt % 2, :]
                emit_rstd(lambda c, tt=tt: X[:, c, tl(tt)], TW, SQ, rs,
                          lambda c, tt=tt: ["X.%d.%d" % (c, tt)], "RSTD.%d" % (tt % 2), 7)
                for c in range(NCH):
                    P.op("dve", lambda e, c=c, tt=tt, rs=rs: e.scalar_tensor_tensor(
                        out=XN[:, c, tl(tt)], in0=X[:, c, tl(tt)], scalar=VEC[:, gcol + c:gcol + c + 1],
                        in1=rs, op0=ALU.mult, op1=ALU.mult),
                        reads=["X.%d.%d" % (c, tt), "RSTD.%d" % (tt % 2), "VEC"],
                        writes=["%s.%d.%d" % (xn_tag, c, tt)])

        def emit_mlp(L):
            o = POFF
            XN = AR.view(o, (NCH, T), BF16); o += 32768
            WR = [AR.view(o + i * 16384, (NCH, 1024), BF16) for i in range(4)]; o += 65536
            H = [AR.view(o + i * 8192, (NCH, TW), BF16) for i in range(2)]; o += 16384
            SQ = AR.view(o, (NCH, TW), BF16); o += 8192
            RSTD = AR.view(o, (2, TW), F32); o += 4096
            T1 = AR.view(o, (2, TW), F32); o += 4096
            emit_norm_bf16(V_NMLP + 8 * L, XN, SQ, RSTD, tiles=[0])

            def load_q(q):
                sa, sb_ = (2 * q) % 4, (2 * q + 1) % 4
                for pc in range(4):
                    P.dma("pool", lambda e, pc=pc: e.dma_start(
                        out=WR[sa][:, :, pc * 256:(pc + 1) * 256],
                        in_=w_up_d[L][:, q * 1024 + pc * 256:q * 1024 + (pc + 1) * 256].rearrange("(kc p) f -> p kc f", p=128)),
                        writes=["WR%d.%d" % (sa, pc)])
                P.dma("pool", lambda e: e.dma_start(
                    out=WR[sb_], in_=w_down_d[L][q * 1024:(q + 1) * 1024, :].rearrange("(fc p) d -> p fc d", p=128)),
                    writes=["WR%d" % sb_])

            load_q(0)
            for q in range(4):
                if q + 1 < 4:
                    load_q(q + 1)
                sa, sb_ = (2 * q) % 4, (2 * q + 1) % 4
                for tt in range(NT):
                    if q == 0 and tt + 1 < NT:
                        emit_norm_bf16(V_NMLP + 8 * L, XN, SQ, RSTD, tiles=[tt + 1])
                    hb = (q * NT + tt) % 2
                    Hb = H[hb]
                    for fc in range(NCH):
                        pb = fc % 2
                        for kc in range(NCH):
                            P.op("pe", lambda e, fc=fc, kc=kc, pb=pb, tt=tt, sa=sa: e.matmul(
                                PS[pb][:], lhsT=WR[sa][:, kc, fc * 128:(fc + 1) * 128], rhs=XN[:, kc, tl(tt)],
                                start=(kc == 0), stop=(kc == NCH - 1)),
                                reads=["WR%d.%d" % (sa, fc // 2), "XN.%d.%d" % (kc, tt)], writes=["PS%d" % pb])
                        P.op("act", lambda e, pb=pb: e.activation(out=T1[:, pb, :], in_=PS[pb][:], func=AF.Relu),
                             reads=["PS%d" % pb], writes=["T1.%d" % pb])
                        P.op("dve", lambda e, pb=pb, fc=fc, Hb=Hb: e.tensor_tensor(
                            out=Hb[:, fc, :], in0=T1[:, pb, :], in1=T1[:, pb, :], op=ALU.mult),
                            reads=["T1.%d" % pb], writes=["H%d.%d" % (hb, fc)])
                    for dc in range(NCH):
                        pb = 2 + dc % 2
                        for fc in range(NCH):
                            P.op("pe", lambda e, fc=fc, dc=dc, pb=pb, Hb=Hb, sb_=sb_: e.matmul(
                                PS[pb][:], lhsT=WR[sb_][:, fc, dc * 128:(dc + 1) * 128], rhs=Hb[:, fc, :],
                                start=(fc == 0), stop=(fc == NCH - 1)),
                                reads=["WR%d" % sb_, "H%d.%d" % (hb, fc)], writes=["PS%d" % pb])
                        P.op("dve", lambda e, dc=dc, pb=pb, tt=tt: e.tensor_tensor(
                            out=X[:, dc, tl(tt)], in0=X[:, dc, tl(tt)], in1=PS[pb][:], op=ALU.add),
                            reads=["PS%d" % pb, "X.%d.%d" % (dc, tt)], writes=["X.%d.%d" % (dc, tt)])
            P.phase_sync()

        def emit_final():
            o = POFF
            SQ = AR.view(o, (NCH, TW), BF16); o += 8192
            RSTD = AR.view(o, (2, TW), F32); o += 4096
            OT = AR.view(o, (4, TW), F32); o += 8192
            osrc = outT_d.rearrange("(c p) t -> p c t", p=128)
            k = 0
            for tt in range(NT):
                rs = RSTD[:, tt % 2, :]
                emit_rstd(lambda c, tt=tt: X[:, c, tl(tt)], TW, SQ, rs,
                          lambda c, tt=tt: ["X.%d.%d" % (c, tt)], "RSTD.%d" % (tt % 2), 7)
                for c in range(NCH):
                    ob = k % 4
                    k += 1
                    P.op("dve", lambda e, c=c, tt=tt, rs=rs, ob=ob: e.scalar_tensor_tensor(
                        out=OT[:, ob, :], in0=X[:, c, tl(tt)], scalar=VEC[:, V_NFIN + c:V_NFIN + c + 1],
                        in1=rs, op0=ALU.mult, op1=ALU.mult),
                        reads=["X.%d.%d" % (c, tt), "RSTD.%d" % (tt % 2), "VEC"], writes=["OT.%d" % ob])
                    P.dma("sp", lambda e, c=c, tt=tt, ob=ob: e.dma_start(out=osrc[:, c, tl(tt)], in_=OT[:, ob, :]),
                          reads=["OT.%d" % ob], writes=["OUT.%d.%d" % (c, tt)])
            P.barrier("sp", ["OUT.%d.%d" % (c, tt) for c in range(NCH) for tt in range(NT)])

        def emit_store_x():
            osrc = outT_d.rearrange("(c p) t -> p c t", p=128)
            for c in range(NCH):
                P.dma("sp", lambda e, c=c: e.dma_start(out=osrc[:, c, :], in_=X[:, c, :]),
                      reads=["X.%d.%d" % (c, tt) for tt in range(NT)], writes=["OUTX.%d" % c])
            P.barrier("sp", ["OUTX.%d" % c for c in range(NCH)])

        def emit_halo_norm(gcol, XH, XNH, SQ, RSH, out_dtype_tag):
            if fused:
                k = exch_count[0]
                exch_count[0] += 1
                hx = nc.dram_tensor("hx%d" % k, [D, HALO], F32).ap()
                hall = nc.dram_tensor("hall%d" % k, [2 * D, HALO], F32).ap()
                P.dma("sp", lambda e: e.dma_start(out=hx.rearrange("(c p) t -> p c t", p=128), in_=X[:, :, T - HALO:T]),
                      reads=["X.%d.%d" % (c, NT - 1) for c in range(NCH)], writes=["hx%d" % k])
                emit_allgather(hx, hall, ["hx%d" % k], ["hall%d" % k])
                hsrc = hall[0:D, :].rearrange("(c p) t -> p c t", p=128)
                P.dma("sp", lambda e: e.dma_start(out=XH, in_=hsrc), reads=["hall%d" % k], writes=["XH"])
            else:
                hsrc = xpT_d.rearrange("(c p) t -> p c t", p=128)
                P.dma("sp", lambda e: e.dma_start(out=XH, in_=hsrc[:, :, T - HALO:T]), writes=["XH"])
            emit_rstd(lambda c: XH[:, c, :], HALO, SQ, RSH, lambda c: ["XH"], "RSH", 7)
            P.op("dve", lambda e: e.tensor_scalar(out=RSH, in0=RSH, scalar1=VEC[:, V_HALF:V_HALF + 1], scalar2=None,
                                                  op0=ALU.mult), reads=["RSH", "VEC"], writes=["RSH"])
            for c in range(NCH):
                P.op("dve", lambda e, c=c: e.scalar_tensor_tensor(
                    out=XNH[:, c, :], in0=XH[:, c, :], scalar=VEC[:, gcol + c:gcol + c + 1],
                    in1=RSH, op0=ALU.mult, op1=ALU.mult),
                    reads=["XH", "RSH", "VEC"], writes=["XNH.%d" % c])

        def emit_conv():
            o = POFF
            XN = AR.view(o, (NCH, T), BF16); o += 32768
            V = AR.view(o, (NCH, T), BF16); o += 32768
            WO = AR.view(o, (NCH, 1024), BF16); o += 16384
            WC = [AR.view(o + i * 6144, (3, NCH, 128), BF16) for i in range(2)]; o += 12288
            Z = [AR.view(o + i * 8256, (1, HALO + T), F32) for i in range(2)]; o += 16512
            SQ = AR.view(o, (NCH, TW), BF16)
            ZC = AR.view(o, (2, TW), F32)
            TH = AR.view(o + 4096, (2, TW), F32); o += 8192
            RSTD = AR.view(o, (2, TW), F32); o += 4096
            XH = AR.view(o, (NCH, HALO), F32); o += 512
            XNH = AR.view(o, (NCH, HALO), BF16); o += 256
            RSH = AR.view(o, (1, HALO), F32)[:, 0, :]; o += 64
            gcol = V_NMIX + 8 * 2
            emit_norm_bf16(gcol, XN, SQ, RSTD)
            emit_halo_norm(gcol, XH, XNH, SQ, RSH, BF16)
            P.phase_sync()
            P.dma("pool", lambda e: e.dma_start(out=WO, in_=conv_w_out_d.rearrange("(kc p) d -> p kc d", p=128)),
                  writes=["WO"])
            wsrc = conv_w_in_d.rearrange("(kc p) f -> p kc f", p=128)

            def load_wc(c):
                for j in range(3):
                    P.dma("pool", lambda e, c=c, j=j: e.dma_start(
                        out=WC[c % 2][:, j, :, :], in_=wsrc[:, :, j * 1024 + c * 128:j * 1024 + (c + 1) * 128]),
                        writes=["WC%d.%d" % (c % 2, j)])

            load_wc(0)
            for c in range(NCH):
                if c + 1 < NCH:
                    load_wc(c + 1)
                W = WC[c % 2]
                wtag = "WC%d" % (c % 2)
                Zb = Z[c % 2][:, 0, :]
                ztag = "Z%d" % (c % 2)
                for j, pb in ((1, 4), (2, 5)):
                    for kc in range(NCH):
                        P.op("pe", lambda e, j=j, pb=pb, kc=kc, W=W: e.matmul(
                            PS[pb][:, 0:HALO], lhsT=W[:, j, kc, :], rhs=XNH[:, kc, :],
                            start=(kc == 0), stop=(kc == NCH - 1)),
                            reads=[wtag + ".%d" % j, "XNH.%d" % kc], writes=["PS%d" % pb])
                P.op("act", lambda e: e.activation(out=TH[:, 0, 0:HALO], in_=PS[5][:, 0:HALO], func=AF.Copy),
                     reads=["PS5"], writes=["TH.0"])
                P.op("dve", lambda e, Zb=Zb: e.tensor_tensor(out=Zb[:, 0:HALO], in0=TH[:, 0, 0:HALO],
                                                             in1=PS[4][:, 0:HALO], op=ALU.mult),
                     reads=["TH.0", "PS4"], writes=[ztag + ".h"])
                for tt in range(NT):
                    pbb = 0 + tt % 2
                    pbc = 2 + tt % 2
                    pbh = 4 + tt % 2
                    for j, pb in ((0, pbb), (1, pbc), (2, pbh)):
                        for kc in range(NCH):
                            P.op("pe", lambda e, j=j, pb=pb, kc=kc, W=W, tt=tt: e.matmul(
                                PS[pb][:], lhsT=W[:, j, kc, :], rhs=XN[:, kc, tl(tt)],
                                start=(kc == 0), stop=(kc == NCH - 1)),
                                reads=[wtag + ".%d" % j, "XN.%d.%d" % (kc, tt)], writes=["PS%d" % pb])
                    tb = tt % 2
                    zs = slice(HALO + tt * TW, HALO + (tt + 1) * TW)
                    P.op("act", lambda e, pbh=pbh, tb=tb: e.activation(out=TH[:, tb, :], in_=PS[pbh][:], func=AF.Copy),
                         reads=["PS%d" % pbh], writes=["TH.%d" % tb])
                    P.op("dve", lambda e, Zb=Zb, zs=zs, tb=tb, pbc=pbc: e.tensor_tensor(
                        out=Zb[:, zs], in0=TH[:, tb, :], in1=PS[pbc][:], op=ALU.mult),
                        reads=["TH.%d" % tb, "PS%d" % pbc], writes=["%s.%d" % (ztag, tt)])
                    prev = ztag + (".h" if tt == 0 else ".%d" % (tt - 1))
                    cw = lambda k, c=c: VEC[:, V_CW + 8 * k + c:V_CW + 8 * k + c + 1]
                    z0 = slice(HALO + tt * TW, HALO + (tt + 1) * TW)
                    z1 = slice(HALO + tt * TW - 1, HALO + (tt + 1) * TW - 1)
                    z2 = slice(HALO + tt * TW - 2, HALO + (tt + 1) * TW - 2)
                    P.op("dve", lambda e, Zb=Zb, z0=z0, tb=tb, cw=cw: e.tensor_scalar(
                        out=ZC[:, tb, :], in0=Zb[:, z0], scalar1=cw(2), scalar2=None, op0=ALU.mult),
                        reads=["%s.%d" % (ztag, tt), "VEC"], writes=["ZC.%d" % tb])
                    P.op("dve", lambda e, Zb=Zb, z1=z1, tb=tb, cw=cw: e.scalar_tensor_tensor(
                        out=ZC[:, tb, :], in0=Zb[:, z1], scalar=cw(1), in1=ZC[:, tb, :], op0=ALU.mult, op1=ALU.add),
                        reads=["%s.%d" % (ztag, tt), prev, "ZC.%d" % tb, "VEC"], writes=["ZC.%d" % tb])
                    P.op("dve", lambda e, Zb=Zb, z2=z2, tb=tb, cw=cw: e.scalar_tensor_tensor(
                        out=ZC[:, tb, :], in0=Zb[:, z2], scalar=cw(0), in1=ZC[:, tb, :], op0=ALU.mult, op1=ALU.add),
                        reads=["%s.%d" % (ztag, tt), prev, "ZC.%d" % tb, "VEC"], writes=["ZC.%d" % tb])
                    P.op("dve", lambda e, c=c, tt=tt, tb=tb, pbb=pbb: e.tensor_tensor(
                        out=V[:, c, tl(tt)], in0=ZC[:, tb, :], in1=PS[pbb][:], op=ALU.mult),
                        reads=["ZC.%d" % tb, "PS%d" % pbb], writes=["V.%d.%d" % (c, tt)])
            for tt in range(NT):
                for dc in range(NCH):
                    pb = 6 + dc % 2
                    for kc in range(NCH):
                        P.op("pe", lambda e, dc=dc, kc=kc, pb=pb, tt=tt: e.matmul(
                            PS[pb][:], lhsT=WO[:, kc, dc * 128:(dc + 1) * 128], rhs=V[:, kc, tl(tt)],
                            start=(kc == 0), stop=(kc == NCH - 1)),
                            reads=["WO", "V.%d.%d" % (kc, tt)], writes=["PS%d" % pb])
                    P.op("dve", lambda e, dc=dc, pb=pb, tt=tt: e.tensor_tensor(
                        out=X[:, dc, tl(tt)], in0=X[:, dc, tl(tt)], in1=PS[pb][:], op=ALU.add),
                        reads=["PS%d" % pb, "X.%d.%d" % (dc, tt)], writes=["X.%d.%d" % (dc, tt)])
            P.phase_sync()

        def emit_pool():
            o = POFF
            PO = AR.view(o, (NCH, T), BF16); o += 32768
            PW = AR.view(o, (4, 2, 256), BF16); o += 4096
            SQ = AR.view(o, (NCH, TW), BF16); o += 8192
            RS = AR.view(o, (1, T), F32)[:, 0, :]; o += 8192
            XF = [AR.view(o + i * 8256, (1, HALO + T), F32)[:, 0, :] for i in range(2)]; o += 16512
            SA = AR.view(o, (1, HALO + T), F32)[:, 0, :]; o += 8256
            SB = AR.view(o, (1, HALO + T), F32)[:, 0, :]; o += 8256
            XH = AR.view(o, (NCH, HALO), F32); o += 512
            XNH = AR.view(o, (NCH, HALO), F32); o += 512
            RSH = AR.view(o, (1, HALO), F32)[:, 0, :]; o += 64
            gcol = V_NMIX + 8 * 1
            P.dma("pool", lambda e: e.dma_start(out=PW, in_=pool_w_d.rearrange("g (ci p) d -> p g ci d", p=128)),
                  writes=["PW"])
            for tt in range(NT):
                emit_rstd(lambda c, tt=tt: X[:, c, tl(tt)], TW, SQ, RS[:, tl(tt)],
                          lambda c, tt=tt: ["X.%d.%d" % (c, tt)], "RS.%d" % tt, 7)
            emit_halo_norm(gcol, XH, XNH, SQ, RSH, F32)
            rs_all = ["RS.%d" % tt for tt in range(NT)]
            for c in range(NCH):
                g = c // 2
                w = 2 << g
                xf = XF[c % 2]
                xtag = "XF%d" % (c % 2)
                P.op("dve", lambda e, c=c, xf=xf: e.scalar_tensor_tensor(
                    out=xf[:, HALO:], in0=X[:, c, :], scalar=VEC[:, gcol + c:gcol + c + 1], in1=RS,
                    op0=ALU.mult, op1=ALU.mult),
                    reads=["X.%d.%d" % (c, tt) for tt in range(NT)] + rs_all + ["VEC"], writes=[xtag])
                P.op("dve", lambda e, c=c, xf=xf: e.tensor_copy(out=xf[:, 0:HALO], in_=XNH[:, c, :]),
                     reads=["XNH.%d" % c], writes=[xtag + "h"])
                src, stag = xf, None
                bufs = [(SA, "SA"), (SB, "SB")]
                s_ = 1
                k = 0
                n = HALO + T
                while s_ < w:
                    dst, dtag = bufs[k % 2]
                    k += 1
                    rd = [xtag, xtag + "h"] if stag is None else [stag]
                    P.op("dve", lambda e, src=src, dst=dst, s_=s_: e.tensor_tensor(
                        out=dst[:, s_:n], in0=src[:, s_:n], in1=src[:, 0:n - s_], op=ALU.add),
                        reads=rd, writes=[dtag])
                    src, stag = dst, dtag
                    s_ *= 2
                P.op("dve", lambda e, src=src, g=g: e.tensor_tensor(
                    out=src[:, HALO:2 * HALO], in0=src[:, HALO:2 * HALO], in1=CORR[:, g * HALO:(g + 1) * HALO],
                    op=ALU.mult), reads=[stag, "CORR"], writes=[stag])
                P.op("dve", lambda e, src=src, c=c, xf=xf, w=w: e.scalar_tensor_tensor(
                    out=PO[:, c, :], in0=src[:, HALO:], scalar=1.0 / w, in1=xf[:, HALO:],
                    op0=ALU.mult, op1=ALU.subtract),
                    reads=[stag, xtag], writes=["PO.%d" % c])
            for tt in range(NT):
                for g in range(4):
                    for dj in range(2):
                        dc = 2 * g + dj
                        pb = dc % 2
                        for ci in range(2):
                            P.op("pe", lambda e, g=g, dj=dj, ci=ci, pb=pb, tt=tt: e.matmul(
                                PS[pb][:], lhsT=PW[:, g, ci, dj * 128:(dj + 1) * 128], rhs=PO[:, 2 * g + ci, tl(tt)],
                                start=(ci == 0), stop=(ci == 1)),
                                reads=["PW", "PO.%d" % (2 * g + ci)], writes=["PS%d" % pb])
                        P.op("dve", lambda e, dc=dc, pb=pb, tt=tt: e.scalar_tensor_tensor(
                            out=X[:, dc, tl(tt)], in0=PS[pb][:], scalar=VEC[:, V_PSC + dc:V_PSC + dc + 1],
                            in1=X[:, dc, tl(tt)], op0=ALU.mult, op1=ALU.add),
                            reads=["PS%d" % pb, "X.%d.%d" % (dc, tt), "VEC"], writes=["X.%d.%d" % (dc, tt)])
            P.phase_sync()


        def emit_attn(ai):
            L = 0 if ai == 0 else 3
            gcol = V_NMIX + 8 * L
            wA = attn_wA_d[ai].rearrange("(kc p) f -> p kc f", p=128)
            AO = AR.view(0, (NCH, T), BF16)
            KT = AR.view(32768, (1, 2 * T), BF16)[:, 0, :]
            VV = AR.view(40960, (32, 128), BF16)
            QT = AR.view(49152, (1, T), BF16)[:, 0, :]
            T2 = AR.view(53248, (2, TW), BF16)
            PT = AR.view(55296, (6, TW), BF16)
            go = 61440
            GS = AR.view(go, (16, 16), F32); go += 1024
            A1 = AR.view(go, (16, 16), F32); go += 1024
            MX = AR.view(go, (16, 8), F32); go += 512
            SBB = AR.view(go, (16, 16), BF16); go += 512
            KM = AR.view(go, (1, 16), F32)[:, 0, :]; go += 64
            KMR = AR.view(go, (1, 16), F32)[:, 0, :]; go += 64
            KMH = AR.view(go, (1, 16), BF16)[:, 0, :]; go += 32
            KML = AR.view(go, (1, 16), BF16)[:, 0, :]; go += 32
            XN = AR.view(65536, (NCH, T), BF16)
            XNP = AR.view(98304, (NCH, T), BF16)
            COS = AR.view(131072, (1, 2 * T), F32)[:, 0, :]
            SIN = AR.view(147456, (1, 2 * T), F32)[:, 0, :]
            XS = AR.view(163840, (NCH, TW), F32)
            SQ = AR.view(180224, (NCH, TW), BF16)
            RSTD = AR.view(188416, (2, TW), F32)
            SELB = AR.view(192512, (1, T), BF16)[:, 0, :]

            rope_ops = []

            def Q(eng, fn, reads=(), writes=()):
                rope_ops.append((False, eng, fn, tuple(reads), tuple(writes)))

            def QD(eng, fn, reads=(), writes=()):
                rope_ops.append((True, eng, fn, tuple(reads), tuple(writes)))

            def rope_emit(n):
                for _ in range(n):
                    if not rope_ops:
                        return
                    isd, eng, fn, r_, w_ = rope_ops.pop(0)
                    (P.dma if isd else P.op)(eng, fn, r_, w_)

            n2 = 2 * T
            ANG = AR.view(0, (1, n2), F32)[:, 0, :]
            AA = AR.view(16384, (1, n2), F32)[:, 0, :]
            POSI = AR.view(16384, (1, n2), I32)[:, 0, :]
            KF = AR.view(32768, (1, n2), F32)[:, 0, :]
            KI = AR.view(49152, (1, n2), I32)[:, 0, :]
            MK = AR.view(49152, (1, n2), F32)[:, 0, :]
            C1 = 6.28125
            C2 = 2 * math.pi - 6.28125
            use_cache = fused and ai > 0 and ("attn0" in need)
            if use_cache:
                QD("sp", lambda e: e.dma_start(out=COS, in_=rope_cache[0]), writes=["COS"])
                QD("sp", lambda e: e.dma_start(out=SIN, in_=rope_cache[1]), writes=["SIN"])
            QD("sp", lambda e: e.dma_start(out=POSI, in_=pos_d), writes=["AA"]) if not use_cache else None
            if not use_cache:
                Q("dve", lambda e: e.tensor_copy(out=ANG, in_=POSI), reads=["AA"], writes=["ANG"])
                Q("dve", lambda e: e.tensor_scalar(out=ANG, in0=ANG, scalar1=VEC[:, V_INVF:V_INVF + 1], scalar2=None,
                                                      op0=ALU.mult), reads=["ANG", "VEC"], writes=["ANG"])
            for shift, OUT, otag in (((0.0, SIN, "SIN"), (math.pi / 2, COS, "COS")) if not use_cache else ()):
                Q("dve", lambda e, shift=shift: e.tensor_scalar(out=AA, in0=ANG, scalar1=shift, scalar2=None,
                                                                   op0=ALU.add), reads=["ANG"], writes=["AA"])
                Q("dve", lambda e: e.tensor_scalar(out=KF, in0=AA, scalar1=1.0 / (2 * math.pi), scalar2=None,
                                                      op0=ALU.mult), reads=["AA"], writes=["KF"])
                Q("dve", lambda e: e.tensor_copy(out=KI, in_=KF), reads=["KF"], writes=["KI"])
                Q("dve", lambda e: e.tensor_copy(out=KF, in_=KI), reads=["KI"], writes=["KF"])
                Q("dve", lambda e: e.scalar_tensor_tensor(out=AA, in0=KF, scalar=-C1, in1=AA, op0=ALU.mult,
                                                             op1=ALU.add), reads=["KF", "AA"], writes=["AA"])
                Q("dve", lambda e: e.scalar_tensor_tensor(out=AA, in0=KF, scalar=-C2, in1=AA, op0=ALU.mult,
                                                             op1=ALU.add), reads=["KF", "AA"], writes=["AA"])
                Q("dve", lambda e: e.tensor_scalar(out=MK, in0=AA, scalar1=math.pi, scalar2=None, op0=ALU.is_gt),
                     reads=["AA", "KI"], writes=["KI"])
                Q("dve", lambda e: e.scalar_tensor_tensor(out=AA, in0=MK, scalar=-2 * math.pi, in1=AA,
                                                             op0=ALU.mult, op1=ALU.add), reads=["KI", "AA"], writes=["AA"])
                Q("dve", lambda e: e.tensor_scalar(out=MK, in0=AA, scalar1=-math.pi, scalar2=None, op0=ALU.is_lt),
                     reads=["AA", "KI"], writes=["KI"])
                Q("dve", lambda e: e.scalar_tensor_tensor(out=AA, in0=MK, scalar=2 * math.pi, in1=AA,
                                                             op0=ALU.mult, op1=ALU.add), reads=["KI", "AA"], writes=["AA"])
                Q("act", lambda e, OUT=OUT: e.activation(out=OUT, in_=AA, func=AF.Sin), reads=["AA"], writes=[otag])
            if not use_cache:
                Q("dve", lambda e: e.tensor_scalar(out=SIN, in0=SIN, scalar1=VEC[:, V_SGN:V_SGN + 1], scalar2=None,
                                                      op0=ALU.mult), reads=["SIN", "VEC"], writes=["SIN"])
                if fused and ("attn1" in need):
                    QD("sp", lambda e: e.dma_start(out=rope_cache[0], in_=COS), reads=["COS"], writes=["cosd"])
                    QD("sp", lambda e: e.dma_start(out=rope_cache[1], in_=SIN), reads=["SIN"], writes=["sind"])
            def norm_stream(xs, XNd, tag):
                HW_ = 256
                XSb = [AR.view(163840 + i * 8192, (NCH, HW_), F32) for i in range(2)]
                for ht in range(2 * NT):
                    b = ht % 2
                    tt = ht // 2
                    t0 = ht * HW_
                    xsb = XSb[b]
                    P.dma("sp", lambda e, t0=t0, xsb=xsb: e.dma_start(out=xsb, in_=xs[:, :, t0:t0 + HW_]),
                          writes=["XS%d" % b])
                    rs = RSTD[:, b, 0:HW_]
                    emit_rstd(lambda c, xsb=xsb: xsb[:, c, :], HW_, SQ[:, :, b * HW_:(b + 1) * HW_], rs,
                              lambda c, b=b: ["XS%d" % b], "RSTD.%d" % b, 7 if b == 0 else 6)
                    for c in range(NCH):
                        P.op("dve", lambda e, c=c, t0=t0, rs=rs, xsb=xsb: e.scalar_tensor_tensor(
                            out=XNd[:, c, t0:t0 + HW_], in0=xsb[:, c, :], scalar=VEC[:, gcol + c:gcol + c + 1],
                            in1=rs, op0=ALU.mult, op1=ALU.mult),
                            reads=["XS%d" % b, "RSTD.%d" % b, "VEC"], writes=["%s.%d.%d" % (tag, c, tt)])
                    rope_emit(2)

            if fused and ai > 0:
                xown_d = nc.dram_tensor("xspill%d" % ai, [D, T], F32).ap()
                xall_d = nc.dram_tensor("xall%d" % ai, [NCH, 256, T], F32).ap()
                xo = xown_d.rearrange("(c p) t -> p c t", p=128)
                for c in range(NCH):
                    P.dma("sp", lambda e, c=c: e.dma_start(out=xo[:, c, :], in_=X[:, c, :]),
                          reads=["X.%d.%d" % (c, tt) for tt in range(NT)], writes=["xspill.%d" % c])
                cc_count[0] += NCH
                kfin = cc_count[0]

                def ag8(e):
                    for c in range(NCH):
                        e.collective_compute("AllGather", ALU.bypass, replica_groups=PAIRS,
                                             ins=[xown_d[c * 128:(c + 1) * 128, :]], outs=[xall_d[c]]).then_inc(ccsem, 1)
                    e.wait_ge(ccsem, kfin)
                    return e.memset(CCS[:], 0.0)

                P.op("pool", ag8, reads=["xspill.%d" % c for c in range(NCH)],
                     writes=["xall.%d" % c for c in range(NCH)])
                emit_norm_bf16(gcol, XN, SQ, RSTD)
                P.phase_sync()
                xprev3 = xall_d[:, 0:128, :].rearrange("c p t -> p c t")
            else:
                xown_d = xT_d
                xprev3 = xpT_d.rearrange("(c p) t -> p c t", p=128)
            if not (fused and ai > 0):
                norm_stream(xown_d.rearrange("(c p) t -> p c t", p=128), XN, "XN")
            norm_stream(xprev3, XNP, "XNP")
            while rope_ops:
                rope_emit(1)
            P.phase_sync()
            import os
            STOP = int(os.environ.get("ATTN_STOP", "99"))
            if STOP <= 1:
                return
            P.op("pool", lambda e: e.memset(SELB, 0.0), writes=["SELB.%d" % q4 for q4 in range(4)])
            WH = [AR.view(163840 + i * 6144, (3, NCH, 128), BF16) for i in range(2)]
            WS = AR.view(163840 + 12288, (2, NCH, 128), BF16)
            RT1s = [AR.view(163840 + 16384 + i * 2048, (1, TW), F32)[:, 0, :] for i in range(2)]
            RT2s = [AR.view(163840 + 20480 + i * 2048, (1, TW), F32)[:, 0, :] for i in range(2)]
            RDEN = AR.view(163840 + 24576, (1, TW), F32)[:, 0, :]
            pj_count = [0]
            scale = 1.0 / math.sqrt(HD)

            def load_wh(h):
                for m in range(3):
                    P.dma("pool", lambda e, h=h, m=m: e.dma_start(
                        out=WH[h % 2][:, m, :, :], in_=wA[:, :, m * 1024 + h * 128:m * 1024 + (h + 1) * 128]),
                        writes=["WH%d.%d" % (h % 2, m)])

            def load_ws(G):
                for m in range(2):
                    P.dma("pool", lambda e, G=G, m=m: e.dma_start(
                        out=WS[:, m, :, :], in_=wA[:, :, 3072 + m * 256 + G * 128:3072 + m * 256 + (G + 1) * 128]),
                        writes=["WS.%d" % m])

            def proj_A(job):
                h, m, dst, dcol, srcXN, srctag, tt, ccol, idx = job
                W = WH[h % 2]
                ba = (5, 4)[idx % 2]
                for kc in range(NCH):
                    P.op("pe", lambda e, kc=kc: e.matmul(PS[ba][:], lhsT=W[:, m, kc, :], rhs=srcXN[:, kc, tl(tt)],
                                                         start=(kc == 0), stop=(kc == NCH - 1)),
                         reads=["WH%d.%d" % (h % 2, m), "%s.%d.%d" % (srctag, kc, tt)], writes=["PS%d" % ba])
                dtag = "QT" if m == 0 else "KT.%d" % (dcol // TW)
                P.op("act", lambda e: e.activation(out=dst[:, dcol:dcol + TW], in_=PS[ba][:], func=AF.Copy),
                     reads=["PS%d" % ba], writes=[dtag])

            def proj_B(job):
                h, m, dst, dcol, srcXN, srctag, tt, ccol, idx = job
                j = h % 4
                ps_ = slice(32 * j, 32 * j + 32)
                ba = (5, 4)[idx % 2]
                bb = (6, 7)[idx % 2]
                rb = idx % 2
                RT1 = RT1s[rb]
                RT2 = RT2s[rb]
                dtag = "QT" if m == 0 else "KT.%d" % (dcol // TW)
                P.op("pe", lambda e: e.matmul(PS[bb][:], lhsT=PERM[:, j * 128:(j + 1) * 128], rhs=dst[:, dcol:dcol + TW],
                                              start=True, stop=True),
                     reads=["PERM", dtag], writes=["PS%d" % bb])
                P.op("dve", lambda e: e.tensor_tensor(out=RT1[ps_, :], in0=PS[ba][ps_, :], in1=COS[ps_, ccol:ccol + TW],
                                                      op=ALU.mult), reads=["PS%d" % ba, "COS", dtag], writes=["RT1.%d" % rb])
                P.op("dve", lambda e: e.tensor_tensor(out=RT2[ps_, :], in0=PS[bb][ps_, :], in1=SIN[ps_, ccol:ccol + TW],
                                                      op=ALU.mult), reads=["PS%d" % bb, "SIN"], writes=["RT2.%d" % rb])
                P.op("dve", lambda e: e.tensor_tensor(out=dst[ps_, dcol:dcol + TW], in0=RT1[ps_, :], in1=RT2[ps_, :],
                                                      op=ALU.add), reads=["RT1.%d" % rb, "RT2.%d" % rb, dtag], writes=[dtag])

            load_wh(0)
            for h in range(NH if STOP > 5 else 1):
                G, j = h // 4, h % 4
                if h + 1 < NH:
                    load_wh(h + 1)
                W = WH[h % 2]
                SUB = os.environ.get("ATTN_SUB", "z")
                if SUB == "a":
                    return
                jobs = []
                for k8 in range(8):
                    src, stag = (XNP, "XNP") if k8 < 4 else (XN, "XN")
                    pj_count[0] += 1
                    jobs.append((h, 1, KT, k8 * TW, src, stag, k8 % 4, k8 * TW, pj_count[0]))
                for tt in range(NT):
                    pj_count[0] += 1
                    jobs.append((h, 0, QT, tt * TW, XN, "XN", tt, (4 + tt) * TW, pj_count[0]))
                proj_A(jobs[0])
                for ji in range(1, len(jobs)):
                    proj_A(jobs[ji])
                    proj_B(jobs[ji - 1])
                proj_B(jobs[-1])
                if SUB <= "c":
                    return
                if STOP <= 2:
                    return
                kt_all = ["KT.%d" % k8 for k8 in range(8)]
                P.op("dve", lambda e: e.tensor_reduce(out=KM, in_=KT.rearrange("p (b k) -> p b k", k=BLK),
                                                      axis=AX.X, op=ALU.add), reads=kt_all, writes=["KM"])
                P.op("dve", lambda e: e.tensor_scalar(out=KMR, in0=KM, scalar1=1.0 / BLK, scalar2=None, op0=ALU.mult),
                     reads=["KM"], writes=["KMR"])
                P.op("dve", lambda e: e.tensor_copy(out=KMH, in_=KMR), reads=["KMR"], writes=["KMH"])
                P.op("dve", lambda e: e.tensor_tensor(out=KML, in0=KMR, in1=KMH, op=ALU.subtract),
                     reads=["KMR", "KMH"], writes=["KML"])
                for qt in range(16):
                    P.op("pe", lambda e, qt=qt: e.matmul(PS[7][:, qt * 16:(qt + 1) * 16],
                                                         lhsT=QT[:, qt * 128:(qt + 1) * 128], rhs=KMH,
                                                         start=True, stop=False),
                         reads=["QT", "KMH"], writes=["PS7"])
                    P.op("pe", lambda e, qt=qt: e.matmul(PS[7][:, qt * 16:(qt + 1) * 16],
                                                         lhsT=QT[:, qt * 128:(qt + 1) * 128], rhs=KML,
                                                         start=False, stop=True),
                         reads=["QT", "KML"], writes=["PS7"])
                P.op("dve", lambda e: e.tensor_tensor(out=GS, in0=PS[7][:, 0:256].rearrange("p (a b) -> p a b", a=16),
                                                      in1=AMASK[:, 0:256].rearrange("p (a b) -> p a b", a=16), op=ALU.add),
                     reads=["PS7", "AMASK"], writes=["GS"])
                for qt in range(16):
                    P.op("dve", lambda e, qt=qt: e.max(out=MX[:, qt, :], in_=GS[:, qt, :]), reads=["GS"],
                         writes=["MX.%d" % qt])
                    P.op("dve", lambda e, qt=qt: e.tensor_scalar(out=A1[:, qt, :], in0=GS[:, qt, :],
                                                                 scalar1=MX[:, qt, 2:3], scalar2=None, op0=ALU.is_ge),
                         reads=["GS", "MX.%d" % qt], writes=["A1"])
                P.op("dve", lambda e: e.tensor_tensor(out=A1, in0=A1, in1=AMASK[:, 256:512].rearrange("p (a b) -> p a b", a=16),
                                                      op=ALU.mult), reads=["A1", "AMASK"], writes=["A1"])
                P.op("dve", lambda e: e.tensor_tensor(out=A1, in0=A1, in1=AMASK[:, 512:768].rearrange("p (a b) -> p a b", a=16),
                                                      op=ALU.add), reads=["A1", "AMASK"], writes=["A1"])
                P.op("dve", lambda e: e.tensor_scalar(out=SBB, in0=A1, scalar1=-1.0, scalar2=BIG, op0=ALU.add,
                                                      op1=ALU.mult), reads=["A1"], writes=["SBB"])
                for i in range(32):
                    src, stag = (XNP, "XNP") if i < 16 else (XN, "XN")
                    to = (i % 16) * 128
                    vb = 3 if (i // 4) % 2 == 0 else 2
                    for kc in range(NCH):
                        P.op("pe", lambda e, kc=kc, i=i, src=src, to=to, W=W, vb=vb: e.matmul(
                            PS[vb][:, (i % 4) * 128:(i % 4 + 1) * 128], lhsT=src[:, kc, to:to + 128], rhs=W[:, 2, kc, :],
                            start=(kc == 0), stop=(kc == NCH - 1)),
                            reads=["WH%d.2" % (h % 2), "%s.%d.%d" % (stag, kc, (i % 16) // 4)], writes=["PS%d" % vb])
                    if i % 4 == 3:
                        g4 = i // 4
                        if False:
                            pass
                        else:
                            P.op("act", lambda e, g4=g4, vb=vb: e.activation(
                                out=VV[:, 4 * g4:4 * g4 + 4, :], in_=PS[vb][:].rearrange("p (a b) -> p a b", a=4),
                                func=AF.Copy), reads=["PS%d" % vb], writes=["VV.%d" % g4])
                for q4 in range(4):
                    for r in range(4):
                        qt = 4 * q4 + r
                        P.op("pe", lambda e, qt=qt, r=r: e.matmul(PS[7][0:16, r * 128:(r + 1) * 128], lhsT=SBB[:, qt, :],
                                                                  rhs=IDENT[:], start=True, stop=True),
                             reads=["SBB", "IDENT"], writes=["PS7"])
                    P.op("act", lambda e, q4=q4: e.activation(out=SELB[0:16, q4 * TW:(q4 + 1) * TW], in_=PS[7][0:16, :],
                                                              func=AF.Copy), reads=["PS7"], writes=["SELB.%d" % q4])
                if STOP <= 3:
                    return
                items = [(tq, kt) for tq in range(NT) for kt in range(16 + 4 * (tq + 1))]
                nit = len(items)

                def emit_S(i):
                    tq, kt = items[i]
                    sbk = (0, 1, 5, 7)[i % 4]
                    m = kt - 16 - 4 * tq
                    diag = m >= 0
                    blk = kt // 2
                    P.op("pe", lambda e: e.matmul(
                        PS[sbk][:], lhsT=LSEL[:, blk * 128:(blk + 1) * 128], rhs=SELB[:, tl(tq)],
                        start=True, stop=False), reads=["LSEL", "SELB.%d" % tq], writes=["PS%d" % sbk])
                    P.op("pe", lambda e: e.matmul(
                        PS[sbk][:], lhsT=KT[:, kt * 128:(kt + 1) * 128], rhs=QT[:, tl(tq)],
                        start=False, stop=(not diag)), reads=["KT.%d" % (kt // 4), "QT"], writes=["PS%d" % sbk])
                    if diag:
                        P.op("pe", lambda e: e.matmul(
                            PS[sbk][:], lhsT=IDENT[:], rhs=CAUS[:, m * TW:(m + 1) * TW],
                            start=False, stop=True), reads=["IDENT", "CAUS"], writes=["PS%d" % sbk])
                    P.op("act", lambda e: e.activation(
                        out=PT[:, i % 6, :], in_=PS[sbk][:], func=AF.Exp, scale=scale),
                        reads=["PS%d" % sbk], writes=["PT.%d" % (i % 6)])

                def emit_PV(i):
                    tq, kt = items[i]
                    nkt = 16 + 4 * (tq + 1)
                    ob = 2 + tq % 2
                    denb = 4 if tq % 2 == 0 else 6
                    pbuf = i % 6
                    P.op("pe", lambda e: e.matmul(
                        PS[ob][:], lhsT=VV[:, kt, :], rhs=PT[:, pbuf, :], start=(kt == 0), stop=(kt == nkt - 1)),
                        reads=["VV.%d" % (kt // 4), "PT.%d" % pbuf], writes=["PS%d" % ob])
                    if kt % 2 == 1:
                        tb = (i // 2) % 2
                        P.op("dve", lambda e: e.tensor_tensor(
                            out=T2[:, tb, :], in0=PT[:, (i - 1) % 6, :], in1=PT[:, pbuf, :], op=ALU.add),
                            reads=["PT.%d" % ((i - 1) % 6), "PT.%d" % pbuf], writes=["T2.%d" % tb])
                        den_pending.append((i + 2, lambda: P.op("pe", lambda e: e.matmul(
                            PS[denb][:], lhsT=ONES[:], rhs=T2[:, tb, :], start=(kt == 1), stop=(kt == nkt - 1)),
                            reads=["ONES", "T2.%d" % tb], writes=["PS%d" % denb])))

                def emit_fin(tq):
                    ob = 2 + tq % 2
                    denb = 4 if tq % 2 == 0 else 6
                    P.op("dve", lambda e: e.reciprocal(out=RDEN, in_=PS[denb][:]), reads=["PS%d" % denb], writes=["RDEN"])
                    P.op("dve", lambda e, h=h: e.tensor_tensor(
                        out=AO[:, h, tl(tq)], in0=PS[ob][:], in1=RDEN, op=ALU.mult),
                        reads=["PS%d" % ob, "RDEN"], writes=["AO.%d.%d" % (h, tq)])

                den_pending = []
                emit_S(0)
                emit_S(1)
                emit_S(2)
                emit_S(3)
                pending = []
                for i in range(nit):
                    emit_PV(i)
                    if i + 4 < nit:
                        emit_S(i + 4)
                    while den_pending and den_pending[0][0] <= i:
                        den_pending.pop(0)[1]()
                    tq, kt = items[i]
                    if kt == 16 + 4 * (tq + 1) - 1:
                        pending.append((i + 4, tq))
                    while pending and pending[0][0] <= i:
                        emit_fin(pending.pop(0)[1])
                while den_pending:
                    den_pending.pop(0)[1]()
                while pending:
                    emit_fin(pending.pop(0)[1])
            if STOP <= 4:
                return
            P.phase_sync()
            XB = AR.view(65536, (NCH, T), F32)
            WO = AR.view(131072, (NCH, 1024), BF16)
            XS2 = AR.view(147456, (NCH, TW), F32)
            xs = xown_d.rearrange("(c p) t -> p c t", p=128)
            P.dma("pool", lambda e: e.dma_start(out=WO, in_=attn_wo_d[ai].rearrange("(h p) d -> p h d", p=128)),
                  writes=["WOA"])
            for tt in range(NT):
                P.dma("sp", lambda e, tt=tt: e.dma_start(out=XS2, in_=xs[:, :, tl(tt)]), writes=["XS2"])
                for dc in range(NCH):
                    pb = dc % 2
                    for h in range(NH):
                        P.op("pe", lambda e, dc=dc, h=h, pb=pb, tt=tt: e.matmul(
                            PS[pb][:], lhsT=WO[:, h, dc * 128:(dc + 1) * 128], rhs=AO[:, h, tl(tt)],
                            start=(h == 0), stop=(h == NH - 1)),
                            reads=["WOA", "AO.%d.%d" % (h, tt)], writes=["PS%d" % pb])
                    P.op("dve", lambda e, dc=dc, pb=pb, tt=tt: e.tensor_tensor(
                        out=XB[:, dc, tl(tt)], in0=XS2[:, dc, :], in1=PS[pb][:], op=ALU.add),
                        reads=["PS%d" % pb, "XS2"], writes=["XB.%d.%d" % (dc, tt)])
            P.phase_sync()
            for c in range(NCH):
                if c % 2 == 0:
                    P.op("dve", lambda e, c=c: e.tensor_copy(out=X[:, c, :], in_=XB[:, c, :]),
                         reads=["XB.%d.%d" % (c, tt) for tt in range(NT)],
                         writes=["X.%d.%d" % (c, tt) for tt in range(NT)])
                else:
                    P.op("act", lambda e, c=c: e.activation(out=X[:, c, :], in_=XB[:, c, :], func=AF.Copy),
                         reads=["XB.%d.%d" % (c, tt) for tt in range(NT)],
                         writes=["X.%d.%d" % (c, tt) for tt in range(NT)])
            P.phase_sync()

        for s_ in steps:
            if s_.startswith("mlp"):
                emit_mlp(int(s_[3:]))
            elif s_.startswith("attn"):
                emit_attn(int(s_[4:]))
            elif s_ == "conv":
                emit_conv()
            elif s_ == "pool":
                emit_pool()
            elif s_ == "final":
                emit_final()
            elif s_ == "storex":
                emit_store_x()
            else:
                raise ValueError(s_)
        P.phase_sync()
        P.emit()
        nc._prog_stats = P.stats
    return nc


def _vec_table(norm_mix, norm_mlp, norm_final, pool_scale, conv_w):
    v = np.zeros((128, NVEC), np.float32)

    def put(col, vec):
        v[:, col:col + 8] = np.asarray(vec, np.float32).reshape(8, 128).T

    for i in range(4):
        put(V_NMIX + 8 * i, norm_mix[i])
        put(V_NMLP + 8 * i, norm_mlp[i])
    put(V_NFIN, norm_final)
    put(V_PSC, pool_scale[0])
    for k in range(3):
        put(V_CW + 8 * k, conv_w[0, k])
    p = np.arange(128)
    v[:, V_INVF] = (500000.0 ** (-(np.arange(0, 32, 2, dtype=np.float32)) / 32.0)).astype(np.float32)[p % 16]
    v[:, V_SGN] = np.where(p % 32 < 16, -1.0, 1.0)
    v[:, V_HALF] = 1.0
    return v


def _corr_table(half):
    c = np.ones((128, 4 * HALO), np.float32)
    if half == 0:
        for g, w in enumerate((2, 4, 8, 16)):
            t = np.arange(HALO)
            c[:, g * HALO:(g + 1) * HALO] = (w / np.minimum(t + 1, w)).astype(np.float32)[None, :]
    return c


def _attn_wA(w_qkv):
    w_qkv = np.asarray(w_qkv, np.float32)
    wq = w_qkv[:, 0:1024].reshape(1024, 8, 128)
    wk = w_qkv[:, 1024:2048].reshape(1024, 8, 128)
    out = np.empty((1024, 3584), np.float32)
    for h in range(8):
        j = h % 4
        perm = np.empty(128, np.int64)
        rot = np.arange(32 * j, 32 * j + 32)
        perm[rot] = np.arange(32)
        others = np.array([p for p in range(128) if not (32 * j <= p < 32 * j + 32)])
        perm[others] = np.arange(32, 128)
        out[:, h * 128:(h + 1) * 128] = wq[:, h, perm]
        out[:, 1024 + h * 128:1024 + (h + 1) * 128] = wk[:, h, perm]
    out[:, 2048:3072] = w_qkv[:, 2048:3072]
    sw = np.concatenate([np.arange(16, 32), np.arange(0, 16)])
    for m, w in enumerate((wq, wk)):
        for G in range(2):
            for j in range(4):
                c0 = 3072 + m * 256 + G * 128 + 32 * j
                out[:, c0:c0 + 32] = w[:, 4 * G + j, sw]
    return out


def _attn_consts(half):
    am = np.zeros((128, 3, 16, 16), np.float32)
    for qt in range(16):
        own = 8 + qt // 2
        for j in range(16):
            valid = (j < own) and (half == 1 or j >= 8)
            am[:, 0, qt, j] = 0.0 if valid else -1e30
            am[:, 1, qt, j] = 1.0 if valid else 0.0
            am[:, 2, qt, j] = 1.0 if j == own else 0.0
    caus = np.zeros((128, 4, TW), np.float32)
    k = np.arange(128)[:, None]
    q = np.arange(TW)[None, :]
    for m in range(4):
        caus[:, m, :] = np.where(128 * m + k <= q, 0.0, -BIG)
    lsel = np.zeros((128, 16, 128), np.float32)
    for j in range(16):
        lsel[j, j, :] = 1.0
    perm = np.zeros((128, 4, 128), np.float32)
    for j in range(4):
        for i in range(32):
            perm[32 * j + (i + 16) % 32, j, 32 * j + i] = 1.0
    return (am.reshape(128, 768), caus.reshape(128, 4 * TW), lsel.reshape(128, 2048),
            np.eye(128, dtype=np.float32), perm.reshape(128, 512))


def run_steps(steps, xT_cores, xpT_cores, inputs, prepared=None, fused=False):
    nc = build_program(steps, fused=fused)
    vec = _vec_table(inputs["norm_mix"], inputs["norm_mlp"], inputs["norm_final"], inputs["pool_scale"],
                     inputs["conv_w"])
    prepared = prepared if prepared is not None else {}
    in_maps = []
    for core in range(8):
        half = core % 2
        b = core // 2
        vc = vec.copy()
        vc[:, V_HALF] = float(half)
        m = {"xT": xT_cores[core], "xpT": xpT_cores[core], "vec": vc, "corr": _corr_table(half)}
        for st_ in steps:
            if st_.startswith("mlp"):
                L = int(st_[3:])
                m["mlp_w_up%d" % L] = inputs["mlp_w_up"][L]
                m["mlp_w_down%d" % L] = inputs["mlp_w_down"][L]
            elif st_.startswith("attn"):
                a = int(st_[4:])
                key = "wA%d" % a
                if key not in prepared:
                    prepared[key] = _attn_wA(inputs["attn_w_qkv"][a])
                m["attn_wA%d" % a] = prepared[key]
                m["attn_wo%d" % a] = inputs["attn_w_o"][a]
                pos = np.asarray(inputs["positions"], np.int32)
                own = pos[b, half * T:(half + 1) * T]
                prev = pos[b, 0:T] if half == 1 else np.zeros(T, np.int32)
                m["pos"] = np.ascontiguousarray(np.broadcast_to(np.concatenate([prev, own])[None, :], (128, 2 * T)))
                am, caus, lsel, ident, perm = _attn_consts(half)
                m["amask"], m["caus"], m["lsel"], m["ident"], m["perm"] = am, caus, lsel, ident, perm
            elif st_ == "conv":
                m["conv_w_in"] = inputs["conv_w_in"][0]
                m["conv_w_out"] = inputs["conv_w_out"][0]
            elif st_ == "pool":
                m["pool_w"] = inputs["pool_w"][0]
        in_maps.append(m)
    res = run_bass_kernel_spmd(nc, in_maps, core_ids=list(range(8)))
    return [np.asarray(r["outT"]) for r in res.results]


ALL_STEPS = ("attn0", "mlp0", "pool", "mlp1", "conv", "mlp2", "attn1", "mlp3", "final")


def kernel(**inputs):
    inputs = {k: np.asarray(v) for k, v in inputs.items()}
    x = inputs["x"].astype(np.float32, copy=False)
    xT = [np.ascontiguousarray(x[c // 2, (c % 2) * T:(c % 2 + 1) * T, :].T) for c in range(8)]
    xpT = [xT[c - 1] if c % 2 == 1 else np.zeros_like(xT[c]) for c in range(8)]
    outs = run_steps(list(ALL_STEPS), xT, xpT, inputs, fused=True)
    out = np.empty((NB, S, D), np.float32)
    for c in range(8):
        out[c // 2, (c % 2) * T:(c % 2 + 1) * T, :] = outs[c].T
    return out
```
